# Optimizing a Trainium2 kernel written in Bass

```python
import math
import jax, jax.numpy as jnp
from jax import lax
import numpy as np


D_MODEL = 1024
BATCH = 4
SEQ = 4096
DEPTH = 4
DEC_BATCH = 16
DEC_SEQ = 2048
PAST_LEN = 128

N_EVEN = (DEPTH + 1) // 2
N_ODD = DEPTH // 2
HY_WIDTH = D_MODEL // 2
HY_SHORT = 3
HY_EMB = 33
HY_BANDS = (HY_EMB - 1) // 2
HY_FILT = 64
HY_INNER = 2
HY_MAX_DECAY = math.log(1e-2) / 0.3
HY_MIN_DECAY = math.log(1e-2) / 1.5
MLA_HEADS = 8
MLA_NOPE = 64
MLA_ROPE = 32
MLA_V = 64
MLA_Q_RANK = 384
MLA_KV_RANK = 256
ROPE_THETA = 10000.0
Q_BLOCK = 128
EVEN_IN = 3 * HY_WIDTH + MLA_Q_RANK + MLA_KV_RANK + MLA_ROPE
EVEN_MIX = HY_WIDTH + MLA_HEADS * MLA_V
CF_WIDTH = D_MODEL // 2
CF_KERNEL = 31
POOL_WIDTH = D_MODEL // 2
POOL_WINDOWS = (2, 4, 8, 16)
POOL_GROUP = POOL_WIDTH // len(POOL_WINDOWS)
ODD_IN = 2 * CF_WIDTH + POOL_WIDTH
ODD_MIX = CF_WIDTH + POOL_WIDTH
D_FF = 4 * D_MODEL
DN_ALPHA = (2 * DEPTH) ** 0.25
DN_BETA = (8 * DEPTH) ** -0.25
LN_EPS = 1e-5
RMS_EPS = 1e-6

kernel_name = 'hyena_mla_conformer_pool_deepnorm_encoder'


def layer_norm(x, g, b):
    xf = x.astype(jnp.float32)
    mu = jnp.mean(xf, axis=-1, keepdims=True)
    var = jnp.mean(jnp.square(xf - mu), axis=-1, keepdims=True)
    return ((xf - mu) * lax.rsqrt(var + LN_EPS) * g + b).astype(x.dtype)


def rms_norm(x, g):
    xf = x.astype(jnp.float32)
    return (xf * lax.rsqrt(jnp.mean(jnp.square(xf), axis=-1, keepdims=True) + RMS_EPS) * g).astype(x.dtype)


def depthwise_conv(x, w, b):
    k, c = w.shape
    y = lax.conv_general_dilated(x, w.reshape(k, 1, c).astype(x.dtype), window_strides=(1,),
                                 padding=[(k // 2, k // 2)],
                                 dimension_numbers=('NWC', 'WIO', 'NWC'),
                                 feature_group_count=c)
    return y + b


def rope_tables(L):
    inv = 1.0 / (ROPE_THETA ** (jnp.arange(0, MLA_ROPE, 2, dtype=jnp.float32) / MLA_ROPE))
    ang = jnp.arange(L, dtype=jnp.float32)[:, None] * inv[None, :]
    return jnp.cos(ang), jnp.sin(ang)


def apply_rope(x, cos, sin):
    x1, x2 = jnp.split(x, 2, axis=-1)
    cos = cos.astype(x.dtype)
    sin = sin.astype(x.dtype)
    return jnp.concatenate([x1 * cos - x2 * sin, x2 * cos + x1 * sin], axis=-1)


def hyena_filter(L, w1, b1, w_inner, b_inner, freq, w_out):
    f32 = jnp.float32
    t = jnp.linspace(0.0, 1.0, L, dtype=f32)[:, None]
    ang = 2.0 * math.pi * jnp.arange(L, dtype=f32)[:, None] / L
    bands = jnp.linspace(1e-4, HY_BANDS - 1, HY_BANDS, dtype=f32)[None, :]
    feats = jnp.concatenate([t, jnp.cos(bands * ang), -jnp.sin(bands * ang)], axis=-1)
    h = jnp.sin(freq * (feats @ w1 + b1))
    for i in range(HY_INNER):
        h = jnp.sin(freq * (h @ w_inner[i] + b_inner[i]))
    k = (h @ w_out).astype(f32)
    deltas = jnp.abs(jnp.linspace(HY_MIN_DECAY, HY_MAX_DECAY, HY_WIDTH, dtype=f32))
    decay = jnp.exp(-t * deltas)
    h_f = k[:, :HY_WIDTH] * decay
    h_b = k[:, HY_WIDTH:] * decay
    return jnp.concatenate([h_f, jnp.zeros((1, HY_WIDTH), f32), h_b[:0:-1]], axis=0)


def hyena_mix(u, conv_w, conv_b, w1, b1, w_inner, b_inner, freq, w_out, skip):
    L = u.shape[1]
    uc = depthwise_conv(u, conv_w, conv_b)
    x0, x1, v = jnp.split(uc, 3, axis=-1)
    z = (x1 * v).astype(jnp.float32)
    g = hyena_filter(L, w1, b1, w_inner, b_inner, freq, w_out)
    n = 2 * L
    zf = jnp.fft.rfft(z, n=n, axis=1)
    gf = jnp.fft.rfft(g, n=n, axis=0)
    y = jnp.fft.irfft(zf * gf[None], n=n, axis=1)[:, :L]
    y = y + z * skip
    return x0 * y.astype(x0.dtype)


def mla_mix(p, q_norm, w_uq, kv_norm, w_ukv):
    B, L, _ = p.shape
    cq, ckv, k_rope = jnp.split(p, [MLA_Q_RANK, MLA_Q_RANK + MLA_KV_RANK], axis=-1)
    q = (rms_norm(cq, q_norm) @ w_uq).reshape(B, L, MLA_HEADS, MLA_NOPE + MLA_ROPE)
    kv = (rms_norm(ckv, kv_norm) @ w_ukv).reshape(B, L, MLA_HEADS, MLA_NOPE + MLA_V)
    q_nope, q_rope = q[..., :MLA_NOPE], q[..., MLA_NOPE:]
    k_nope, v = kv[..., :MLA_NOPE], kv[..., MLA_NOPE:]
    cos, sin = rope_tables(L)
    q_rope = apply_rope(q_rope, cos[:, None, :], sin[:, None, :])
    k_rope = apply_rope(k_rope, cos, sin)
    scale = (MLA_NOPE + MLA_ROPE) ** -0.5
    nb = L // Q_BLOCK

    def to_blocks(a):
        return jnp.moveaxis(a.reshape(B, nb, Q_BLOCK, *a.shape[2:]), 1, 0)

    def attend(qs):
        qn, qr = qs
        s = (jnp.einsum('bqhd,bkhd->bhqk', qn, k_nope).astype(jnp.float32)
             + jnp.einsum('bqhr,bkr->bhqk', qr, k_rope).astype(jnp.float32))
        pr = jax.nn.softmax(s * scale, axis=-1)
        return jnp.einsum('bhqk,bkhd->bqhd', pr.astype(v.dtype), v)

    o = lax.map(attend, (to_blocks(q_nope), to_blocks(q_rope)))
    return jnp.moveaxis(o, 0, 1).reshape(B, L, MLA_HEADS * MLA_V)


def conformer_mix(u, dw_w, dw_b, ln_g, ln_b):
    a, gate = jnp.split(u, 2, axis=-1)
    h = a * jax.nn.sigmoid(gate)
    h = depthwise_conv(h, dw_w, dw_b)
    h = layer_norm(h, ln_g, ln_b)
    return jax.nn.silu(h)


def pool_mix(u, pool_w, pool_scale):
    B, L, _ = u.shape
    uf = u.astype(jnp.float32)
    cs = jnp.concatenate([jnp.zeros((B, 1, POOL_WIDTH), jnp.float32), jnp.cumsum(uf, axis=1)], axis=1)
    pos = jnp.arange(L)
    outs = []
    for gi, w in enumerate(POOL_WINDOWS):
        lo = w // 2
        hi = w - 1 - lo
        start = jnp.clip(pos - lo, 0, L)
        end = jnp.clip(pos + hi + 1, 0, L)
        sl = slice(gi * POOL_GROUP, (gi + 1) * POOL_GROUP)
        c = cs[..., sl]
        mean = (jnp.take(c, end, axis=1) - jnp.take(c, start, axis=1)) / (end - start).astype(jnp.float32)[None, :, None]
        d = (mean - uf[..., sl]).astype(u.dtype)
        outs.append(d @ pool_w[gi])
    return jnp.concatenate(outs, axis=-1) * pool_scale


def even_mixer(x, w, i):
    p = x @ w['ev_w_in'][i]
    hy = hyena_mix(p[..., :3 * HY_WIDTH], w['hy_conv_w'][i], w['hy_conv_b'][i],
                   w['hy_filt_w1'][i], w['hy_filt_b1'][i], w['hy_filt_w_inner'][i],
                   w['hy_filt_b_inner'][i], w['hy_filt_freq'][i], w['hy_filt_w_out'][i],
                   w['hy_skip'][i])
    at = mla_mix(p[..., 3 * HY_WIDTH:], w['mla_q_norm'][i], w['mla_w_uq'][i],
                 w['mla_kv_norm'][i], w['mla_w_ukv'][i])
    return jnp.concatenate([hy, at], axis=-1) @ w['ev_w_out'][i]


def odd_mixer(x, w, i):
    p = x @ w['od_w_in'][i] + w['od_b_in'][i]
    cf = conformer_mix(p[..., :2 * CF_WIDTH], w['cf_dw_w'][i], w['cf_dw_b'][i],
                       w['cf_ln_g'][i], w['cf_ln_b'][i])
    pl = pool_mix(p[..., 2 * CF_WIDTH:], w['pool_w'][i], w['pool_scale'][i])
    return jnp.concatenate([cf, pl], axis=-1) @ w['od_w_out'][i] + w['od_b_out'][i]


def trunk(x, w):
    for layer in range(DEPTH):
        i = layer // 2
        if layer % 2 == 0:
            mix = even_mixer(x, w, i)
        else:
            mix = odd_mixer(x, w, i)
        x = layer_norm(DN_ALPHA * x + mix, w['ln1_g'][layer], w['ln1_b'][layer])
        h = jnp.square(jax.nn.relu(x @ w['mlp_w1'][layer])) @ w['mlp_w2'][layer]
        x = layer_norm(DN_ALPHA * x + h, w['ln2_g'][layer], w['ln2_b'][layer])
    return x


def setup_inputs(seed: int = 0) -> dict:
    key = jax.random.key(seed)
    ks = iter(jax.random.split(key, 64))

    def nrm(shape, scale):
        return scale * jax.random.normal(next(ks), shape, jnp.float32)

    def gain(shape):
        return 1.0 + nrm(shape, 0.01)

    return {
        'x_prompt': nrm((BATCH, SEQ, D_MODEL), 1.0),
        'x_sample': nrm((DEC_BATCH, DEC_SEQ, D_MODEL), 1.0),
        'ev_w_in': nrm((N_EVEN, D_MODEL, EVEN_IN), D_MODEL ** -0.5),
        'hy_conv_w': nrm((N_EVEN, HY_SHORT, 3 * HY_WIDTH), HY_SHORT ** -0.5),
        'hy_conv_b': nrm((N_EVEN, 3 * HY_WIDTH), 0.01),
        'hy_filt_w1': nrm((N_EVEN, HY_EMB, HY_FILT), HY_EMB ** -0.5),
        'hy_filt_b1': nrm((N_EVEN, HY_FILT), 0.1),
        'hy_filt_w_inner': nrm((N_EVEN, HY_INNER, HY_FILT, HY_FILT), HY_FILT ** -0.5),
        'hy_filt_b_inner': nrm((N_EVEN, HY_INNER, HY_FILT), 0.1),
        'hy_filt_freq': 1.0 + nrm((N_EVEN, HY_FILT), 0.1),
        'hy_filt_w_out': nrm((N_EVEN, HY_FILT, 2 * HY_WIDTH), HY_FILT ** -0.5),
        'hy_skip': nrm((N_EVEN, HY_WIDTH), 1.0),
        'mla_q_norm': gain((N_EVEN, MLA_Q_RANK)),
        'mla_w_uq': nrm((N_EVEN, MLA_Q_RANK, MLA_HEADS * (MLA_NOPE + MLA_ROPE)), MLA_Q_RANK ** -0.5),
        'mla_kv_norm': gain((N_EVEN, MLA_KV_RANK)),
        'mla_w_ukv': nrm((N_EVEN, MLA_KV_RANK, MLA_HEADS * (MLA_NOPE + MLA_V)), MLA_KV_RANK ** -0.5),
        'ev_w_out': nrm((N_EVEN, EVEN_MIX, D_MODEL), EVEN_MIX ** -0.5 * DN_BETA),
        'od_w_in': nrm((N_ODD, D_MODEL, ODD_IN), D_MODEL ** -0.5),
        'od_b_in': nrm((N_ODD, ODD_IN), 0.01),
        'cf_dw_w': nrm((N_ODD, CF_KERNEL, CF_WIDTH), CF_KERNEL ** -0.5),
        'cf_dw_b': nrm((N_ODD, CF_WIDTH), 0.01),
        'cf_ln_g': gain((N_ODD, CF_WIDTH)),
        'cf_ln_b': nrm((N_ODD, CF_WIDTH), 0.01),
        'pool_w': nrm((N_ODD, len(POOL_WINDOWS), POOL_GROUP, POOL_GROUP), POOL_GROUP ** -0.5),
        'pool_scale': 1.0 + nrm((N_ODD, POOL_WIDTH), 0.1),
        'od_w_out': nrm((N_ODD, ODD_MIX, D_MODEL), ODD_MIX ** -0.5 * DN_BETA),
        'od_b_out': nrm((N_ODD, D_MODEL), 0.01),
        'ln1_g': gain((DEPTH, D_MODEL)),
        'ln1_b': nrm((DEPTH, D_MODEL), 0.01),
        'mlp_w1': nrm((DEPTH, D_MODEL, D_FF), D_MODEL ** -0.5),
        'mlp_w2': nrm((DEPTH, D_FF, D_MODEL), D_FF ** -0.5 * DN_BETA),
        'ln2_g': gain((DEPTH, D_MODEL)),
        'ln2_b': nrm((DEPTH, D_MODEL), 0.01),
    }


def reference(x_prompt, x_sample, ev_w_in, hy_conv_w, hy_conv_b, hy_filt_w1, hy_filt_b1,
              hy_filt_w_inner, hy_filt_b_inner, hy_filt_freq, hy_filt_w_out, hy_skip,
              mla_q_norm, mla_w_uq, mla_kv_norm, mla_w_ukv, ev_w_out,
              od_w_in, od_b_in, cf_dw_w, cf_dw_b, cf_ln_g, cf_ln_b, pool_w, pool_scale,
              od_w_out, od_b_out, ln1_g, ln1_b, mlp_w1, mlp_w2, ln2_g, ln2_b):
    w = {
        'ev_w_in': ev_w_in, 'hy_conv_w': hy_conv_w, 'hy_conv_b': hy_conv_b,
        'hy_filt_w1': hy_filt_w1, 'hy_filt_b1': hy_filt_b1,
        'hy_filt_w_inner': hy_filt_w_inner, 'hy_filt_b_inner': hy_filt_b_inner,
        'hy_filt_freq': hy_filt_freq, 'hy_filt_w_out': hy_filt_w_out, 'hy_skip': hy_skip,
        'mla_q_norm': mla_q_norm, 'mla_w_uq': mla_w_uq, 'mla_kv_norm': mla_kv_norm,
        'mla_w_ukv': mla_w_ukv, 'ev_w_out': ev_w_out,
        'od_w_in': od_w_in, 'od_b_in': od_b_in, 'cf_dw_w': cf_dw_w, 'cf_dw_b': cf_dw_b,
        'cf_ln_g': cf_ln_g, 'cf_ln_b': cf_ln_b, 'pool_w': pool_w, 'pool_scale': pool_scale,
        'od_w_out': od_w_out, 'od_b_out': od_b_out,
        'ln1_g': ln1_g, 'ln1_b': ln1_b, 'mlp_w1': mlp_w1, 'mlp_w2': mlp_w2,
        'ln2_g': ln2_g, 'ln2_b': ln2_b,
    }
    y_prompt = trunk(x_prompt, w)
    y_sample = trunk(x_sample, w)
    return (y_prompt, y_sample)
```

```python
import contextlib
import math
import numpy as np
import ml_dtypes
import concourse.bass as bass
import concourse.mybir as mybir
from concourse.bass_utils import run_bass_kernel_spmd

F32 = mybir.dt.float32
BF16 = mybir.dt.bfloat16
AF = mybir.ActivationFunctionType
ALU = mybir.AluOpType

D = 1024
NT = 8192
SEQS = [(0, 4096), (4096, 2048), (6144, 2048)]
GROUPS = [(4096, [0]), (2048, [4096, 6144])]
DEPTH = 4
ALPHA = float((2 * DEPTH) ** 0.25)
LN_EPS = 1e-5
RMS_EPS = 1e-6
HY_MAX_DECAY = math.log(1e-2) / 0.3
HY_MIN_DECAY = math.log(1e-2) / 1.5
SEM_LIMIT = 24000
import os as _os
NO_ACT_DMA = bool(_os.environ.get("KDBG_NOACTDMA"))
MAXOPS = int(_os.environ.get("KDBG_MAXOPS", "1000000000"))
PI = float(np.pi)


class Buf:
    __slots__ = ("t", "wr", "rd", "name", "multi", "psum")

    def __init__(self, t, name="", multi=False, psum=False):
        self.psum = psum
        self.t = t
        self.wr = {}
        self.rd = {}
        self.name = name
        self.multi = multi

    def __getitem__(self, k):
        return self.t[k]


def _add(dct, tok):
    s, v = tok
    k = id(s)
    if k not in dct or dct[k][1] < v:
        dct[k] = tok


class Rot:
    def __init__(self, bufs):
        self.b = list(bufs)
        self.i = 0

    def next(self):
        b = self.b[self.i % len(self.b)]
        self.i += 1
        return b


class FW:
    def __init__(self, nc):
        self.nc = nc
        self.eng = {"pe": nc.tensor, "act": nc.scalar, "dve": nc.vector,
                    "pool": nc.gpsimd, "sp": nc.sync}
        self.sem = {}
        self.cnt = {}
        self.nsem = 0
        self.retired = []
        for e in ("pe", "act", "dve", "pool"):
            self._new_sem(e)
        self.waited = {e: {} for e in self.eng}
        self.dq = {}
        for q in ("sp", "act", "pool"):
            n = 8 if q == "sp" else 4
            self.dq[q] = {"sems": [self._mk_sem(f"d{q}{i}") for i in range(n)],
                          "cnt": [0] * n, "rr": 0}
        self.n_ins = {e: 0 for e in self.eng}

    def _mk_sem(self, name):
        self.nsem += 1
        return self.nc.alloc_semaphore(f"{name}_{self.nsem}")

    def _new_sem(self, e):
        if e in self.sem:
            self.retired.append((self.sem[e], self.cnt[e]))
        self.sem[e] = self._mk_sem(f"s{e}")
        self.cnt[e] = 0

    def sb(self, name, shape, dt):
        return Buf(self.nc.alloc_sbuf_tensor(name, list(shape), dt), name)

    def ps(self, name, shape, dt=F32):
        return Buf(self.nc.alloc_psum_tensor(name, list(shape), dt), name, psum=True)

    def dram(self, name, shape, dt, kind="Internal"):
        return Buf(self.nc.dram_tensor(name, list(shape), dt, kind=kind), name, multi=True)

    def _wait(self, e, deps):
        eng = self.eng[e]
        w = self.waited[e]
        best = {}
        for d in deps:
            if d is None:
                continue
            s, v = d
            k = id(s)
            if w.get(k, 0) >= v:
                continue
            if k not in best or best[k][1] < v:
                best[k] = (s, v)
        for k, (s, v) in best.items():
            eng.wait_ge(s, v)
            w[k] = v

    def _deps(self, reads, writes):
        deps = []
        for b in reads:
            deps.extend(b.wr.values())
            if b.psum:
                deps.extend(b.rd.values())
        for b in writes:
            deps.extend(b.rd.values())
            if not b.multi:
                deps.extend(b.wr.values())
        return deps

    def _commit(self, tok, reads, writes):
        for b in writes:
            if b.multi:
                _add(b.wr, tok)
            else:
                b.wr = {}
                b.rd = {}
                _add(b.wr, tok)
        for b in reads:
            if not any(b is x for x in writes):
                _add(b.rd, tok)

    def op(self, e, fn, reads=(), writes=()):
        self.nops = getattr(self, "nops", 0) + 1
        if self.nops > MAXOPS:
            return None
        if self.nops == MAXOPS:
            import traceback
            print("LAST OP", e, [b.name for b in reads], [b.name for b in writes], traceback.extract_stack()[-2].lineno)
        deps = self._deps(reads, writes)
        if e == "pe":
            own = id(self.sem[e])
            deps = [d for d in deps if id(d[0]) != own]
        self._wait(e, deps)
        ins = fn()
        if self.cnt[e] >= SEM_LIMIT:
            self._new_sem(e)
        self.cnt[e] += 1
        ins.then_inc(self.sem[e], 1)
        tok = (self.sem[e], self.cnt[e])
        self._commit(tok, reads, writes)
        self.n_ins[e] += 1
        return tok

    def dma(self, q, out_ap, in_ap, reads=(), writes=(), **kw):
        if q == "act" and NO_ACT_DMA:
            q = "sp"
        self.nops = getattr(self, "nops", 0) + 1
        if self.nops > MAXOPS:
            return None
        if self.nops == MAXOPS:
            import traceback
            print("LAST DMA", q, [b.name for b in reads], [b.name for b in writes], traceback.extract_stack()[-2].lineno)
        deps = self._deps(reads, writes)
        d = self.dq[q]
        i = d["rr"]
        d["rr"] = (i + 1) % len(d["sems"])
        if d["cnt"][i] * 16 >= SEM_LIMIT:
            self._wait(q, [(d["sems"][i], d["cnt"][i] * 16)])
            self.retired.append((d["sems"][i], d["cnt"][i] * 16))
            d["sems"][i] = self._mk_sem(f"d{q}{i}")
            d["cnt"][i] = 0
        s = d["sems"][i]
        if d["cnt"][i]:
            deps.append((s, d["cnt"][i] * 16))
        self._wait(q, deps)
        ins = self.eng[q].dma_start(out=out_ap, in_=in_ap, **kw)
        d["cnt"][i] += 1
        ins.then_inc(s, 16)
        tok = (s, d["cnt"][i] * 16)
        self._commit(tok, reads, writes)
        self.n_ins[q] += 1
        return tok

    def all_tokens(self):
        toks = list(self.retired)
        for e in ("pe", "act", "dve", "pool"):
            if self.cnt[e]:
                toks.append((self.sem[e], self.cnt[e]))
        for q, d in self.dq.items():
            for s, c in zip(d["sems"], d["cnt"]):
                if c:
                    toks.append((s, c * 16))
        return toks

    def barrier(self):
        toks = self.all_tokens()
        for e in ("pe", "act", "dve", "pool", "sp"):
            self._wait(e, toks)


class Phase:
    _uid = [0]

    def __init__(self, K, tag):
        self.K = K
        Phase._uid[0] += 1
        self.tag = f"{tag}u{Phase._uid[0]}"
        self.st = contextlib.ExitStack()
        self.n = 0

    def sb(self, name, shape, dt):
        self.n += 1
        t = self.st.enter_context(self.K.nc.sbuf_tensor(f"{self.tag}_{name}_{self.n}", list(shape), dt))
        return Buf(t, name)

    def close(self):
        self.K.barrier()
        self.st.close()


def _bf(a):
    return np.ascontiguousarray(a.astype(ml_dtypes.bfloat16))


_CONST = {}


def host_consts():
    if _CONST:
        return _CONST
    c = {}
    c["ident"] = np.eye(128, dtype=np.float32)
    for L in (4096, 2048):
        n = 2 * L
        idx = np.arange(L, dtype=np.int64)
        prod = (idx[:, None] * idx[None, :]) % n
        ang = prod.astype(np.float64) * (2.0 * np.pi / n)
        fc = np.cos(ang)
        fs = -np.sin(ang)
        sgn = np.where(idx % 2 == 0, 1.0, -1.0)
        fs[:, 0] = sgn
        c[f"fc{L}"] = _bf(fc)
        c[f"fs{L}"] = _bf(fs)
        c[f"fsi{L}"] = _bf(fs.T)
        del ang, prod, fc, fs
        t = np.linspace(0.0, 1.0, L, dtype=np.float32).astype(np.float64)
        angp = 2.0 * np.pi * np.arange(L, dtype=np.float64) / L
        bands = np.linspace(1e-4, 15.0, 16, dtype=np.float32).astype(np.float64)
        feats = np.concatenate([t[None, :], np.cos(bands[:, None] * angp[None, :]),
                                -np.sin(bands[:, None] * angp[None, :])], axis=0)
        c[f"feats{L}"] = np.ascontiguousarray(feats.astype(np.float32))
        c[f"negt{L}"] = np.ascontiguousarray((-t).astype(np.float32).reshape(L // 128, 128).T)
    deltas = np.abs(np.linspace(HY_MIN_DECAY, HY_MAX_DECAY, 512, dtype=np.float32))
    c["deltas"] = np.ascontiguousarray(np.broadcast_to(deltas[None, :], (128, 512)).astype(np.float32))
    inv = 1.0 / (10000.0 ** (np.arange(0, 32, 2, dtype=np.float32) / 32.0))
    ang = np.arange(4096, dtype=np.float32)[:, None] * inv[None, :].astype(np.float32)
    c["rope"] = np.ascontiguousarray(np.concatenate([np.cos(ang), np.sin(ang)], axis=1).astype(np.float32))
    pe = np.ones((4, 16), np.float32)
    for gi, w in enumerate((2, 4, 8, 16)):
        lo = w // 2
        hi = w - 1 - lo
        for t_ in range(8):
            pe[gi, t_] = 1.0 / (min(t_ + hi, 10 ** 9) - max(t_ - lo, 0) + 1)
        for r in range(8):
            pe[gi, 8 + r] = 1.0 / (min(r, hi) + lo + 1)
    c["pooledge"] = np.ascontiguousarray(np.broadcast_to(pe.reshape(1, 64), (128, 64)).astype(np.float32))
    _CONST.update(c)
    return _CONST


WEIGHT_SHAPES = {
    'ev_w_in': (2, 1024, 2208), 'hy_conv_w': (2, 3, 1536), 'hy_conv_b': (2, 1536),
    'hy_filt_w1': (2, 33, 64), 'hy_filt_b1': (2, 64), 'hy_filt_w_inner': (2, 2, 64, 64),
    'hy_filt_b_inner': (2, 2, 64), 'hy_filt_freq': (2, 64), 'hy_filt_w_out': (2, 64, 1024),
    'hy_skip': (2, 512), 'mla_q_norm': (2, 384), 'mla_w_uq': (2, 384, 768),
    'mla_kv_norm': (2, 256), 'mla_w_ukv': (2, 256, 1024), 'ev_w_out': (2, 1024, 1024),
    'od_w_in': (2, 1024, 1536), 'od_b_in': (2, 1536), 'cf_dw_w': (2, 31, 512),
    'cf_dw_b': (2, 512), 'cf_ln_g': (2, 512), 'cf_ln_b': (2, 512), 'pool_w': (2, 4, 128, 128),
    'pool_scale': (2, 512), 'od_w_out': (2, 1024, 1024), 'od_b_out': (2, 1024),
    'ln1_g': (4, 1024), 'ln1_b': (4, 1024), 'mlp_w1': (4, 1024, 4096), 'mlp_w2': (4, 4096, 1024),
    'ln2_g': (4, 1024), 'ln2_b': (4, 1024),
}
CONST_SHAPES = {
    'ident': ((128, 128), F32), 'fc4096': ((4096, 4096), BF16), 'fs4096': ((4096, 4096), BF16),
    'fc2048': ((2048, 2048), BF16), 'fs2048': ((2048, 2048), BF16),
    'fsi4096': ((4096, 4096), BF16), 'fsi2048': ((2048, 2048), BF16),
    'feats4096': ((33, 4096), F32), 'feats2048': ((33, 2048), F32),
    'negt4096': ((128, 32), F32), 'negt2048': ((128, 16), F32), 'deltas': ((128, 512), F32),
    'rope': ((4096, 32), F32), 'pooledge': ((128, 64), F32),
}


def wlead(name, nlayers):
    full = WEIGHT_SHAPES[name][0]
    if name.startswith(("ev_", "hy_", "mla_")):
        need = (nlayers + 1) // 2
    elif name.startswith(("od_", "cf_", "pool_")):
        need = max(1, nlayers // 2)
    else:
        need = nlayers
    return min(full, max(1, need))


def seq_of(tok):
    for s0, L in SEQS:
        if s0 <= tok < s0 + L:
            return s0, L
    raise ValueError(tok)


class Prog:
    def __init__(self, debug=False, nlayers=DEPTH, stop=None):
        self.debug = debug
        self.nlayers = nlayers
        self.stop = stop
        nc = bass.Bass("TRN2", target_bir_lowering=False)
        self.nc = nc
        K = FW(nc)
        self.K = K
        self.I = {}
        self.I["xin"] = K.dram("xin", [NT, D], F32, kind="ExternalInput")
        for n, s in WEIGHT_SHAPES.items():
            self.I[n] = K.dram(n, [wlead(n, nlayers)] + list(s[1:]), F32, kind="ExternalInput")
        for n, (s, dt) in CONST_SHAPES.items():
            self.I[n] = K.dram(n, list(s), dt, kind="ExternalInput")
        self.y = K.dram("y", [NT, D], F32, kind="ExternalOutput")
        dbg = set(debug) if debug else set()
        self.S = {}
        for n, s, dt in [("xA", [NT, D], F32), ("xB", [NT, D], F32), ("xT", [D, NT], BF16),
                         ("u", [1536, NT], F32), ("QT", [8, 128, NT], BF16), ("KT", [8, 128, NT], BF16),
                         ("V", [NT, 1024], BF16), ("mixT", [D, NT], BF16),
                         ("x0c", [512, NT], F32), ("zf", [512, NT], BF16), ("zT", [NT, 512], BF16),
                         ("AB", [4096, 1024], BF16), ("hgl", [512, NT], BF16),
                         ("xM", [NT, D], F32), ("xmT", [D, NT], BF16)]:
            self.S[n] = K.dram("s_" + n, s, dt, kind=("ExternalOutput" if n in dbg else "Internal"))
        self.identf = K.sb("identf", [128, 128], F32)
        self.identb = K.sb("identb", [128, 128], BF16)
        self.PSB = [K.ps(f"psb{i}", [128, 512], F32) for i in range(8)]
        K.dma("sp", self.identf[:, :], self.I["ident"][:, :], [self.I["ident"]], [self.identf])
        K.op("dve", lambda: nc.vector.tensor_copy(out=self.identb[:, :], in_=self.identf[:, :]),
             [self.identf], [self.identb])
        self.cval = {}
        for nm, v in [("eps_ln", LN_EPS), ("eps_rms", RMS_EPS), ("zero", 0.0), ("one", 1.0)]:
            b = K.sb("c_" + nm, [128, 1], F32)
            K.op("dve", (lambda b=b, v=v: nc.vector.memset(b[:, :], v)), [], [b])
            self.cval[nm] = b
        self.WB = {}
        self.cast_weights()
        K.barrier()
        self.body()
        K.barrier()

    def cast_weights(self):
        K = self.K
        order = []
        for l in range(self.nlayers):
            i = l // 2
            if l % 2 == 0:
                order += [("ev_w_in", i, 1024, 2208), ("ev_w_out", i, 1024, 1024)]
            else:
                order += [("od_w_in", i, 1024, 1536), ("od_w_out", i, 1024, 1024), ("pool_w", i, 512, 128)]
            order += [("mlp_w1", l, 1024, 4096), ("mlp_w2", l, 4096, 1024)]
        for name, i, rows, cols in order:
            dst = K.dram(f"wb_{name}_{i}", [rows, cols], BF16)
            self.WB[(name, i)] = dst
            src = self.I[name]
            rb = 128
            for r0 in range(0, rows, rb):
                if name == "pool_w":
                    sap = src.t[i].rearrange("g a b -> (g a) b")[r0:r0 + rb, :]
                else:
                    sap = src.t[i, r0:r0 + rb, :]
                K.dma("pool", dst.t[r0:r0 + rb, :], sap, [src], [dst])

    def bcast_row(self, ph, name, src_buf, src_ap_row, n):
        t = ph.sb(name, [128, n], F32)
        self.K.dma("sp", t[:, :], src_ap_row.broadcast_to([128, n]), [src_buf], [t])
        return t

    def col_load(self, ph, name, src_buf, src_ap, shape):
        t = ph.sb(name, shape, F32)
        full = t.t[tuple(slice(None) for _ in shape)]
        self.K.dma("sp", full, src_ap, [src_buf], [t], allow_slow_non_contiguous=True)
        return t

    def evac(self, i, out_ap, in_ap, reads, writes):
        nc, K = self.nc, self.K
        if i % 2 == 0:
            K.op("act", lambda: nc.scalar.copy(out=out_ap, in_=in_ap), reads, writes)
        else:
            K.op("dve", lambda: nc.vector.tensor_copy(out=out_ap, in_=in_ap), reads, writes)

    def layer_norm_tile(self, ph, tmp, src_fp32, g_t, b_t, out_fp32, n, tag):
        nc, K = self.nc, self.K
        st, mv, rs, nm = tmp["st"], tmp["mv"], tmp["rs"], tmp["nm"]
        nch = n // 512

        def f_stats():
            for c in range(nch):
                ins = nc.vector.bn_stats(out=st[:, c * 6:(c + 1) * 6], in_=src_fp32[:, c * 512:(c + 1) * 512])
            return ins
        K.op("dve", f_stats, [src_fp32], [st])
        K.op("dve", lambda: nc.vector.bn_aggr(out=mv[:, :], in_=st[:, 0:6 * nch]), [st], [mv])
        K.op("act", lambda: nc.scalar.activation(out=rs[:, :], in_=mv[:, 1:2], func=AF.Sqrt,
                                                 bias=self.cval["eps_ln"][:, :], scale=1.0),
             [mv, self.cval["eps_ln"]], [rs])
        K.op("dve", lambda: nc.vector.reciprocal(out=rs[:, :], in_=rs[:, :]), [rs], [rs])
        K.op("dve", lambda: nc.vector.scalar_tensor_tensor(out=nm[:, :], in0=mv[:, 0:1], scalar=-1.0, in1=rs[:, :],
                                                           op0=ALU.mult, op1=ALU.mult), [mv, rs], [nm])
        K.op("act", lambda: nc.scalar.activation(out=out_fp32[:, 0:n], in_=src_fp32[:, 0:n], func=AF.Identity,
                                                 bias=nm[:, :], scale=rs[:, :]), [src_fp32, nm, rs], [out_fp32])
        K.op("dve", lambda: nc.vector.tensor_mul(out=out_fp32[:, 0:n], in0=out_fp32[:, 0:n], in1=g_t[:, 0:n]),
             [out_fp32, g_t], [out_fp32])
        K.op("dve", lambda: nc.vector.tensor_add(out=out_fp32[:, 0:n], in0=out_fp32[:, 0:n], in1=b_t[:, 0:n]),
             [out_fp32, b_t], [out_fp32])

    def transpose_to(self, src_bf, ncol_chunks, ps, dst_ap_fn, dst_buf, ev_i):
        nc, K = self.nc, self.K
        psv = ps.t[:, :].bitcast(BF16)

        def f():
            for c in range(ncol_chunks):
                ins = nc.tensor.transpose(out=psv[:, c * 128:(c + 1) * 128], in_=src_bf[:, c * 128:(c + 1) * 128],
                                          identity=self.identb[:, :])
            return ins
        K.op("pe", f, [src_bf, self.identb], [ps])
        in_ap = psv[:, 0:ncol_chunks * 128].rearrange("p (c t) -> p c t", t=128)
        self.evac(ev_i, dst_ap_fn(), in_ap, [ps], [dst_buf])

    def body(self):
        S = self.S
        self.phase_x0()
        if self.stop == "x0":
            return
        xcur = self.I["xin"]
        for l in range(self.nlayers):
            i = l // 2
            last = (l == self.nlayers - 1)
            xnext = self.y if last else (S["xA"] if l % 2 == 0 else S["xB"])
            if l % 2 == 0:
                self.phase_even_in(i)
                if self.stop and self.stop.startswith(f"ea{l}"):
                    return
                for L, starts in GROUPS:
                    self.phase_hy_filter(i, L)
                    self.phase_hy_zprep(i, L, starts)
                    for cg in range(2):
                        self.phase_hy_dft(i, L, starts, cg)
                if self.stop == f"eb{l}":
                    return
                for s0, L in SEQS:
                    self.phase_attn(s0, L)
                if self.stop == f"ec{l}":
                    return
            else:
                self.phase_odd_in(i)
                if self.stop == f"oa{l}":
                    return
                self.phase_conformer(i)
                self.phase_pool(i)
                if self.stop == f"ob{l}":
                    return
            self.phase_tail1(l, xcur)
            self.phase_tail2(l, xnext, last)
            xcur = xnext
            if self.stop == f"t{l}":
                return

    def phase_x0(self):
        nc, K, S = self.nc, self.K, self.S
        ph = Phase(K, "x0")
        xt = Rot([ph.sb(f"x{i}", [128, 1024], F32) for i in range(6)])
        xb = Rot([ph.sb(f"xb{i}", [128, 1024], BF16) for i in range(8)])
        xts = Rot([ph.sb(f"xts{i}", [128, 8, 512], BF16) for i in range(2)])
        psr = Rot(self.PSB[0:4])
        xin = self.I["xin"]
        ev = 0
        for b in range(NT // 512):
            xbs = []
            for j in range(4):
                t0 = b * 512 + j * 128
                xf = xt.next()
                K.dma("sp", xf[:, :], xin.t[t0:t0 + 128, :], [xin], [xf])
                xbb = xb.next()
                if j % 2 == 0:
                    K.op("dve", lambda xbb=xbb, xf=xf: nc.vector.tensor_copy(out=xbb[:, :], in_=xf[:, :]), [xf], [xbb])
                else:
                    K.op("act", lambda xbb=xbb, xf=xf: nc.scalar.copy(out=xbb[:, :], in_=xf[:, :]), [xf], [xbb])
                xbs.append(xbb)
            dst = xts.next()
            self.xT_block(xbs, dst, psr, ev)
            ev += 8
            K.dma("act", S["xT"].t[:, b * 512:(b + 1) * 512].rearrange("(k p) t -> p k t", p=128), dst[:, :, :],
                  [dst], [S["xT"]])
        ph.close()

    def xT_block(self, xbs, dst, psr, ev):
        nc, K = self.nc, self.K
        for k in range(8):
            ps = psr.next()
            psv = ps.t[:, :].bitcast(BF16)

            def f(k=k, psv=psv):
                for j in range(4):
                    ins = nc.tensor.transpose(out=psv[:, j * 128:(j + 1) * 128], in_=xbs[j][:, k * 128:(k + 1) * 128],
                                              identity=self.identb[:, :])
                return ins
            K.op("pe", f, list(xbs) + [self.identb], [ps])
            self.evac(ev + k, dst[:, k, :], psv[:, 0:512], [ps], [dst])

    def phase_even_in(self, i):
        nc, K, S, I = self.nc, self.K, self.S, self.I
        ph = Phase(K, f"ea{i}")
        PSB = self.PSB
        win = ph.sb("win", [128, 8, 2208], BF16)
        wb = self.WB[("ev_w_in", i)]
        for k in range(8):
            K.dma("sp", win[:, k, :], wb.t[k * 128:(k + 1) * 128, :], [wb], [win])
        qg = self.col_load(ph, "qg", I["mla_q_norm"], I["mla_q_norm"].t[i].rearrange("(c p) -> p c", p=128), [128, 3])
        kg = self.col_load(ph, "kg", I["mla_kv_norm"], I["mla_kv_norm"].t[i].rearrange("(c p) -> p c", p=128), [128, 2])
        stg = ph.sb("stg", [128, 1024], F32)
        wuq = ph.sb("wuq", [128, 3, 768], BF16)
        wukv = ph.sb("wukv", [128, 2, 1024], BF16)
        for c in range(3):
            src = I["mla_w_uq"].t[i, c * 128:(c + 1) * 128, :].rearrange("p (h e) -> p h e", e=96)
            K.dma("sp", stg[:, 0:512].rearrange("p (h d) -> p h d", d=64), src[:, :, 0:64], [I["mla_w_uq"]], [stg])
            K.dma("sp", stg[:, 512:768].rearrange("p (h d) -> p h d", d=32), src[:, :, 64:96], [I["mla_w_uq"]], [stg])
            K.op("dve", lambda c=c: nc.vector.tensor_scalar(out=wuq[:, c, :], in0=stg[:, 0:768], scalar1=qg[:, c:c + 1],
                                                            scalar2=None, op0=ALU.mult), [stg, qg], [wuq])
        for c in range(2):
            src = I["mla_w_ukv"].t[i, c * 128:(c + 1) * 128, :].rearrange("p (h e) -> p h e", e=128)
            K.dma("sp", stg[:, 0:512].rearrange("p (h d) -> p h d", d=64), src[:, :, 0:64], [I["mla_w_ukv"]], [stg])
            K.dma("sp", stg[:, 512:1024].rearrange("p (h d) -> p h d", d=64), src[:, :, 64:128], [I["mla_w_ukv"]], [stg])
            K.op("dve", lambda c=c: nc.vector.tensor_scalar(out=wukv[:, c, :], in0=stg[:, 0:1024], scalar1=kg[:, c:c + 1],
                                                            scalar2=None, op0=ALU.mult), [stg, kg], [wukv])
        if self.stop == "ea0a":
            ph.close()
            return
        xTb = Rot([ph.sb(f"xTb{j}", [128, 8, 512], BF16) for j in range(2)])
        ust = Rot([ph.sb(f"ust{j}", [128, 512], F32) for j in range(3)])
        QTs = Rot([ph.sb(f"QTs{j}", [128, 8, 512], BF16) for j in range(2)])
        KTs = Rot([ph.sb(f"KTs{j}", [128, 8, 512], BF16) for j in range(2)])
        ropet = Rot([ph.sb(f"rope{j}", [128, 32], F32) for j in range(2)])
        junk = ph.sb("junk", [128, 384], F32)
        ssq = ph.sb("ssq", [128, 1], F32)
        ssk = ph.sb("ssk", [128, 1], F32)
        rq = ph.sb("rq", [128, 1], F32)
        rk = ph.sb("rk", [128, 1], F32)
        cqn = ph.sb("cqn", [128, 384], BF16)
        ckvn = ph.sb("ckvn", [128, 256], BF16)
        latT = ph.sb("latT", [128, 5, 128], BF16)
        qtm = Rot([ph.sb(f"qtm{j}", [128, 8, 128], BF16) for j in range(2)])
        ktm = Rot([ph.sb(f"ktm{j}", [128, 8, 128], BF16) for j in range(2)])
        vst = Rot([ph.sb(f"vst{j}", [128, 8, 128], BF16) for j in range(2)])
        rt = [ph.sb(f"rt{j}", [128, 8, 16], F32) for j in range(4)]
        kt4 = [ph.sb(f"kt4{j}", [128, 16], F32) for j in range(4)]
        for b_ in qtm.b:
            K.op("dve", lambda b_=b_: nc.vector.memset(b_[:, :, 96:128], 0.0), [], [b_])
        for b_ in ktm.b:
            K.op("dve", lambda b_=b_: nc.vector.memset(b_[:, :, 96:128], 0.0), [], [b_])
            K.op("dve", lambda b_=b_: nc.vector.memset(b_[:, :, 96:97], 1.0), [b_], [b_])
        for b_ in vst.b:
            K.op("dve", lambda b_=b_: nc.vector.memset(b_[:, :, 64:128], 1.0), [], [b_])
        hyps = Rot(PSB[0:2])
        ev = 0
        nblk = NT // 512
        nxt = xTb.next()
        K.dma("sp", nxt[:, :, :], S["xT"].t[:, 0:512].rearrange("(k p) t -> p k t", p=128), [S["xT"]], [nxt])
        B0 = int(_os.environ.get("KDBG_BLK0", "0"))
        if B0:
            nxt = xTb.next()
            K.dma("sp", nxt[:, :, :], S["xT"].t[:, B0 * 512:B0 * 512 + 512].rearrange("(k p) t -> p k t", p=128), [S["xT"]], [nxt])
        for b in range(B0, nblk):
            T0 = b * 512
            xtb = nxt
            if b + 1 < nblk:
                nxt = xTb.next()
                K.dma("sp", nxt[:, :, :], S["xT"].t[:, T0 + 512:T0 + 1024].rearrange("(k p) t -> p k t", p=128),
                      [S["xT"]], [nxt])
            for m in range(12):
                ps = hyps.next()

                def f(m=m, ps=ps):
                    for k in range(8):
                        ins = nc.tensor.matmul(ps[:, :], win[:, k, m * 128:(m + 1) * 128], xtb[:, k, :],
                                               start=(k == 0), stop=(k == 7))
                    return ins
                K.op("pe", f, [win, xtb], [ps])
                u_ = ust.next()
                self.evac(ev, u_[:, :], ps[:, :], [ps], [u_])
                ev += 1
                K.dma("act" if m % 2 else "sp", S["u"].t[m * 128:(m + 1) * 128, T0:T0 + 512], u_[:, :], [u_], [S["u"]])
            if self.stop == "ea0b":
                ph.close()
                return
            qts, kts = QTs.next(), KTs.next()
            for j in range(4):
                if (self.stop == "ea0c" and j == 1) or (self.stop == "ea0c2" and j == 2) or (self.stop == "ea0c3" and j == 3):
                    ph.close()
                    return
                tok = T0 + j * 128
                s0, L = seq_of(tok)
                pos = tok - s0
                rp = ropet.next()
                K.dma("sp", rp[:, :], I["rope"].t[pos:pos + 128, :], [I["rope"]], [rp])
                cos_b = rp.t[:, 0:16].unsqueeze(1).broadcast_to([128, 8, 16])
                sin_b = rp.t[:, 16:32].unsqueeze(1).broadcast_to([128, 8, 16])
                psA, psB_ = PSB[2], PSB[3]

                def f(j=j):
                    for k in range(8):
                        nc.tensor.matmul(psA[:, 0:384], xtb[:, k, j * 128:(j + 1) * 128], win[:, k, 1536:1920],
                                         start=(k == 0), stop=(k == 7))
                    for k in range(8):
                        ins = nc.tensor.matmul(psB_[:, 0:288], xtb[:, k, j * 128:(j + 1) * 128], win[:, k, 1920:2208],
                                               start=(k == 0), stop=(k == 7))
                    return ins
                K.op("pe", f, [win, xtb], [psA, psB_])
                K.op("dve", lambda: nc.vector.memset(ssq[:, :], 0.0), [], [ssq])
                K.op("dve", lambda: nc.vector.memset(ssk[:, :], 0.0), [], [ssk])
                K.op("act", lambda: nc.scalar.activation(out=junk[:, 0:384], in_=psA[:, 0:384], func=AF.Square,
                                                         accum_out=ssq[:, :]), [psA, ssq], [junk, ssq])
                K.op("act", lambda: nc.scalar.activation(out=junk[:, 0:256], in_=psB_[:, 0:256], func=AF.Square,
                                                         accum_out=ssk[:, :]), [psB_, ssk], [junk, ssk])
                K.op("act", lambda: nc.scalar.activation(out=rq[:, :], in_=ssq[:, :], func=AF.Sqrt,
                                                         bias=self.cval["eps_rms"][:, :], scale=1.0 / 384.0),
                     [ssq, self.cval["eps_rms"]], [rq])
                K.op("act", lambda: nc.scalar.activation(out=rk[:, :], in_=ssk[:, :], func=AF.Sqrt,
                                                         bias=self.cval["eps_rms"][:, :], scale=1.0 / 256.0),
                     [ssk, self.cval["eps_rms"]], [rk])
                K.op("dve", lambda: nc.vector.reciprocal(out=rq[:, :], in_=rq[:, :]), [rq], [rq])
                K.op("dve", lambda: nc.vector.reciprocal(out=rk[:, :], in_=rk[:, :]), [rk], [rk])
                K.op("act", lambda: nc.scalar.activation(out=cqn[:, :], in_=psA[:, 0:384], func=AF.Copy, scale=rq[:, :]),
                     [psA, rq], [cqn])
                K.op("act", lambda: nc.scalar.activation(out=ckvn[:, :], in_=psB_[:, 0:256], func=AF.Copy, scale=rk[:, :]),
                     [psB_, rk], [ckvn])
                kt_ = ktm.next()
                x1, x2 = psB_[:, 256:272], psB_[:, 272:288]
                cs, sn = rp[:, 0:16], rp[:, 16:32]
                K.op("dve", lambda: nc.vector.tensor_tensor(out=kt4[0][:, :], in0=x1, in1=cs, op=ALU.mult), [psB_, rp, ckvn], [kt4[0]])
                K.op("dve", lambda: nc.vector.tensor_tensor(out=kt4[1][:, :], in0=x2, in1=sn, op=ALU.mult), [psB_, rp, ckvn], [kt4[1]])
                K.op("dve", lambda: nc.vector.tensor_tensor(out=kt4[2][:, :], in0=x2, in1=cs, op=ALU.mult), [psB_, rp, ckvn], [kt4[2]])
                K.op("dve", lambda: nc.vector.tensor_tensor(out=kt4[3][:, :], in0=x1, in1=sn, op=ALU.mult), [psB_, rp, ckvn], [kt4[3]])
                b0 = kt4[0].t[:, :].unsqueeze(1).broadcast_to([128, 8, 16])
                b1 = kt4[1].t[:, :].unsqueeze(1).broadcast_to([128, 8, 16])
                b2 = kt4[2].t[:, :].unsqueeze(1).broadcast_to([128, 8, 16])
                b3 = kt4[3].t[:, :].unsqueeze(1).broadcast_to([128, 8, 16])
                K.op("dve", lambda kt_=kt_: nc.vector.tensor_tensor(out=kt_[:, :, 64:80], in0=b0, in1=b1, op=ALU.subtract), [kt4[0], kt4[1]], [kt_])
                K.op("dve", lambda kt_=kt_: nc.vector.tensor_tensor(out=kt_[:, :, 80:96], in0=b2, in1=b3, op=ALU.add), [kt4[2], kt4[3], kt_], [kt_])
                psT = PSB[4]
                psTv = psT.t[:, :].bitcast(BF16)

                def f():
                    for c in range(3):
                        nc.tensor.transpose(out=psTv[:, c * 128:(c + 1) * 128], in_=cqn[:, c * 128:(c + 1) * 128],
                                            identity=self.identb[:, :])
                    for c in range(2):
                        ins = nc.tensor.transpose(out=psTv[:, (3 + c) * 128:(4 + c) * 128], in_=ckvn[:, c * 128:(c + 1) * 128],
                                                  identity=self.identb[:, :])
                    return ins
                K.op("pe", f, [cqn, ckvn, self.identb], [psT])
                self.evac(ev, latT[:, :, :], psTv[:, 0:640].rearrange("p (c t) -> p c t", t=128), [psT], [latT])
                ev += 1
                psQ1, psQ2 = PSB[5], PSB[6]

                def f():
                    for c in range(3):
                        nc.tensor.matmul(psQ1[:, 0:512], latT[:, c, :], wuq[:, c, 0:512], start=(c == 0), stop=(c == 2))
                    for c in range(3):
                        ins = nc.tensor.matmul(psQ2[:, 0:256], latT[:, c, :], wuq[:, c, 512:768], start=(c == 0), stop=(c == 2))
                    return ins
                K.op("pe", f, [latT, wuq], [psQ1, psQ2])
                qt_ = qtm.next()
                K.op("act", lambda qt_=qt_: nc.scalar.copy(out=qt_[:, :, 0:64], in_=psQ1[:, 0:512].rearrange("p (h d) -> p h d", d=64)),
                     [psQ1], [qt_])
                q2 = psQ2[:, 0:256].rearrange("p (h r) -> p h r", r=32)
                K.op("dve", lambda: nc.vector.tensor_tensor(out=rt[0][:, :, :], in0=q2[:, :, 0:16], in1=cos_b, op=ALU.mult), [psQ2, rp], [rt[0]])
                K.op("dve", lambda: nc.vector.tensor_tensor(out=rt[1][:, :, :], in0=q2[:, :, 16:32], in1=sin_b, op=ALU.mult), [psQ2, rp], [rt[1]])
                K.op("dve", lambda: nc.vector.tensor_tensor(out=rt[2][:, :, :], in0=q2[:, :, 16:32], in1=cos_b, op=ALU.mult), [psQ2, rp], [rt[2]])
                K.op("dve", lambda: nc.vector.tensor_tensor(out=rt[3][:, :, :], in0=q2[:, :, 0:16], in1=sin_b, op=ALU.mult), [psQ2, rp], [rt[3]])
                K.op("dve", lambda qt_=qt_: nc.vector.tensor_tensor(out=qt_[:, :, 64:80], in0=rt[0][:, :, :], in1=rt[1][:, :, :], op=ALU.subtract),
                     [rt[0], rt[1], qt_], [qt_])
                K.op("dve", lambda qt_=qt_: nc.vector.tensor_tensor(out=qt_[:, :, 80:96], in0=rt[2][:, :, :], in1=rt[3][:, :, :], op=ALU.add),
                     [rt[2], rt[3], qt_], [qt_])
                def f():
                    for c in range(2):
                        nc.tensor.matmul(psQ1[:, 0:512], latT[:, 3 + c, :], wukv[:, c, 0:512], start=(c == 0), stop=(c == 1))
                    for c in range(2):
                        ins = nc.tensor.matmul(psQ2[:, 0:512], latT[:, 3 + c, :], wukv[:, c, 512:1024], start=(c == 0), stop=(c == 1))
                    return ins
                K.op("pe", f, [latT, wukv], [psQ1, psQ2])
                K.op("act", lambda kt_=kt_: nc.scalar.copy(out=kt_[:, :, 0:64], in_=psQ1[:, 0:512].rearrange("p (h d) -> p h d", d=64)),
                     [psQ1, kt_], [kt_])
                v_ = vst.next()
                K.op("dve", lambda v_=v_: nc.vector.tensor_copy(out=v_[:, :, 0:64], in_=psQ2[:, 0:512].rearrange("p (h d) -> p h d", d=64)), [psQ2, v_], [v_])
                K.dma("act", S["V"].t[tok:tok + 128, :], v_[:, :, :].rearrange("p h d -> p (h d)"), [v_], [S["V"]])
                for src, dstb, psx in ((qt_, qts, PSB[7]), (kt_, kts, PSB[4])):
                    psv = psx.t[:, :].bitcast(BF16)

                    def f(src=src, psv=psv):
                        for h in range(8):
                            ins = nc.tensor.transpose(out=psv[:, h * 128:(h + 1) * 128], in_=src[:, h, :],
                                                      identity=self.identb[:, :])
                        return ins
                    K.op("pe", f, [src, self.identb], [psx])
                    self.evac(ev, dstb[:, :, j * 128:(j + 1) * 128], psv[:, :].rearrange("p (h t) -> p h t", t=128),
                              [psx], [dstb])
                    ev += 1
            K.dma("sp", S["QT"].t[:, :, T0:T0 + 512].rearrange("h r t -> r h t"), qts[:, :, :], [qts], [S["QT"]])
            K.dma("act", S["KT"].t[:, :, T0:T0 + 512].rearrange("h r t -> r h t"), kts[:, :, :], [kts], [S["KT"]])
            if self.stop == "ea0d" or (self.stop == "ea0e" and b == int(_os.environ.get("KDBG_BLK", "3"))):
                ph.close()
                return
        ph.close()

    def sin_layer(self, ph, ps, n, freq, fb, fbbuf, out_ap, out_buf, tmp):
        nc, K = self.nc, self.K
        pre, cm = tmp
        K.op("dve", lambda: nc.vector.tensor_scalar(out=pre[0:64, 0:n], in0=ps[0:64, 0:n], scalar1=freq[0:64, 0:1],
                                                    scalar2=fb, op0=ALU.mult, op1=ALU.add), [ps, freq, fbbuf], [pre])
        K.op("dve", lambda: nc.vector.tensor_single_scalar(out=cm[0:64, 0:n], in_=pre[0:64, 0:n], scalar=PI, op=ALU.is_gt), [pre], [cm])
        K.op("dve", lambda: nc.vector.scalar_tensor_tensor(out=pre[0:64, 0:n], in0=cm[0:64, 0:n], scalar=-2 * PI, in1=pre[0:64, 0:n],
                                                           op0=ALU.mult, op1=ALU.add), [pre, cm], [pre])
        K.op("dve", lambda: nc.vector.tensor_single_scalar(out=cm[0:64, 0:n], in_=pre[0:64, 0:n], scalar=-PI, op=ALU.is_lt), [pre], [cm])
        K.op("dve", lambda: nc.vector.scalar_tensor_tensor(out=pre[0:64, 0:n], in0=cm[0:64, 0:n], scalar=2 * PI, in1=pre[0:64, 0:n],
                                                           op0=ALU.mult, op1=ALU.add), [pre, cm], [pre])
        K.op("act", lambda: nc.scalar.activation(out=out_ap, in_=pre[0:64, 0:n], func=AF.Sin, bias=self.cval["zero"][0:64, :], scale=1.0),
             [pre, self.cval["zero"]], [out_buf])

    def phase_hy_filter(self, i, L):
        nc, K, S, I = self.nc, self.K, self.S, self.I
        ph = Phase(K, f"hf{i}_{L}")
        PSB = self.PSB
        feats = ph.sb("feats", [33, L], F32)
        K.dma("sp", feats[:, :], I[f"feats{L}"].t[:, :], [I[f"feats{L}"]], [feats])
        w1 = ph.sb("w1", [33, 64], F32)
        K.dma("sp", w1[:, :], I["hy_filt_w1"].t[i], [I["hy_filt_w1"]], [w1])
        wi = ph.sb("wi", [64, 2, 64], F32)
        K.dma("sp", wi[:, :, :], I["hy_filt_w_inner"].t[i].rearrange("l a b -> a l b"), [I["hy_filt_w_inner"]], [wi])
        wo = ph.sb("wo", [64, 1024], F32)
        K.dma("sp", wo[:, :], I["hy_filt_w_out"].t[i], [I["hy_filt_w_out"]], [wo])
        freq = self.col_load(ph, "freq", I["hy_filt_freq"], I["hy_filt_freq"].t[i].rearrange("(p o) -> p o", o=1), [64, 1])
        bb = ph.sb("bb", [64, 3], F32)
        K.dma("sp", bb[:, 0:1], I["hy_filt_b1"].t[i].rearrange("(p o) -> p o", o=1), [I["hy_filt_b1"]], [bb], allow_slow_non_contiguous=True)
        K.dma("sp", bb[:, 1:3], I["hy_filt_b_inner"].t[i].rearrange("l p -> p l"), [I["hy_filt_b_inner"]], [bb], allow_slow_non_contiguous=True)
        fb = ph.sb("fb", [64, 3], F32)
        K.op("dve", lambda: nc.vector.tensor_scalar(out=fb[:, :], in0=bb[:, :], scalar1=freq[:, 0:1], scalar2=None, op0=ALU.mult),
             [bb, freq], [fb])
        negt = ph.sb("negt", [128, L // 128], F32)
        K.dma("sp", negt[:, :], I[f"negt{L}"].t[:, :], [I[f"negt{L}"]], [negt])
        dl = ph.sb("dl", [128, 512], F32)
        K.dma("sp", dl[:, :], I["deltas"].t[:, :], [I["deltas"]], [dl])
        hA = ph.sb("hA", [64, L], F32)
        hB = ph.sb("hB", [64, L], F32)
        tmp = (ph.sb("pre", [64, 512], F32), ph.sb("cm", [64, 512], F32))
        psr = Rot(PSB[0:2])
        srcs = [feats, hA, hB]
        dsts = [hA, hB, hA]
        for li in range(3):
            src, dst = srcs[li], dsts[li]
            for blk in range(L // 512):
                ps = psr.next()
                if li == 0:
                    K.op("pe", lambda ps=ps, blk=blk: nc.tensor.matmul(ps[0:64, :], w1[:, :], feats[:, blk * 512:(blk + 1) * 512],
                                                                      start=True, stop=True), [w1, feats], [ps])
                else:
                    K.op("pe", lambda ps=ps, blk=blk, src=src, li=li: nc.tensor.matmul(ps[0:64, :], wi[:, li - 1, :], src[:, blk * 512:(blk + 1) * 512],
                                                                                        start=True, stop=True), [wi, src], [ps])
                self.sin_layer(ph, ps, 512, freq, fb[0:64, li:li + 1], fb, dst[:, blk * 512:(blk + 1) * 512], dst, tmp)
        h3 = hA
        dec = ph.sb("dec", [128, 512], F32)
        hf = ph.sb("hf", [128, 512], F32)
        hb = ph.sb("hb", [128, 512], F32)
        sA = Rot([ph.sb(f"sA{j}", [128, 1024], BF16) for j in range(2)])
        sc = 2.0 / (2 * L)
        for ti in range(L // 128):
            p1, p2 = PSB[2 + 2 * (ti % 2)], PSB[3 + 2 * (ti % 2)]

            def f(ti=ti, p1=p1, p2=p2):
                nc.tensor.matmul(p1[:, :], h3[:, ti * 128:(ti + 1) * 128], wo[:, 0:512], start=True, stop=True)
                return nc.tensor.matmul(p2[:, :], h3[:, ti * 128:(ti + 1) * 128], wo[:, 512:1024], start=True, stop=True)
            K.op("pe", f, [h3, wo], [p1, p2])
            K.op("act", lambda ti=ti: nc.scalar.activation(out=dec[:, :], in_=dl[:, :], func=AF.Exp, scale=negt[:, ti:ti + 1]),
                 [dl, negt], [dec])
            K.op("dve", lambda p1=p1: nc.vector.scalar_tensor_tensor(out=hf[:, :], in0=p1[:, :], scalar=sc, in1=dec[:, :], op0=ALU.mult, op1=ALU.mult), [p1, dec], [hf])
            K.op("dve", lambda p2=p2: nc.vector.scalar_tensor_tensor(out=hb[:, :], in0=p2[:, :], scalar=sc, in1=dec[:, :], op0=ALU.mult, op1=ALU.mult), [p2, dec], [hb])
            if ti == 0:
                K.op("dve", lambda: nc.vector.memset(hb[0:1, :], 0.0), [hb], [hb])
            ab = sA.next()
            K.op("dve", lambda ab=ab: nc.vector.tensor_add(out=ab[:, 0:512], in0=hf[:, :], in1=hb[:, :]), [hf, hb], [ab])
            K.op("dve", lambda ab=ab: nc.vector.tensor_sub(out=ab[:, 512:1024], in0=hf[:, :], in1=hb[:, :]), [hf, hb, ab], [ab])
            K.dma("sp", S["AB"].t[ti * 128:(ti + 1) * 128, :], ab[:, :], [ab], [S["AB"]])
        ph.close()

    def phase_hy_zprep(self, i, L, starts):
        nc, K, S, I = self.nc, self.K, self.S, self.I
        ph = Phase(K, f"hz{i}_{L}")
        PSB = self.PSB
        cw = self.col_load(ph, "cw", I["hy_conv_w"], I["hy_conv_w"].t[i].rearrange("j (m p) -> p j m", p=128), [128, 3, 12])
        cb = self.col_load(ph, "cb", I["hy_conv_b"], I["hy_conv_b"].t[i].rearrange("(m p) -> p m", p=128), [128, 12])
        SEG = 2048
        rows = Rot([ph.sb(f"rows{j}", [128, SEG + 2], F32) for j in range(3)])
        cv = Rot([ph.sb(f"cv{j}", [128, SEG], F32) for j in range(3)])
        zfb = Rot([ph.sb(f"zfb{j}", [128, SEG], BF16) for j in range(2)])
        zst = Rot([ph.sb(f"zst{j}", [128, 16, 512], BF16) for j in range(2)])
        psr = Rot(PSB[0:4])
        ev = 0

        def conv(m, s0, g0):
            r = rows.next()
            lo = g0 - 1
            hi = g0 + SEG + 1
            a = max(lo, s0)
            b_ = min(hi, s0 + L)
            if a > lo:
                K.op("dve", lambda: nc.vector.memset(r[:, 0:1], 0.0), [], [r])
            if b_ < hi:
                K.op("dve", lambda: nc.vector.memset(r[:, SEG + 1:SEG + 2], 0.0), [], [r])
            K.dma("sp", r[:, a - lo:b_ - lo], S["u"].t[m * 128:(m + 1) * 128, a:b_], [S["u"]], [r])
            o = cv.next()
            K.op("act", lambda: nc.scalar.activation(out=o[:, :], in_=r[:, 1:SEG + 1], func=AF.Identity,
                                                     bias=cb[:, m:m + 1], scale=cw[:, 1, m:m + 1]), [r, cb, cw], [o])
            K.op("dve", lambda: nc.vector.scalar_tensor_tensor(out=o[:, :], in0=r[:, 0:SEG], scalar=cw[:, 0, m:m + 1], in1=o[:, :],
                                                               op0=ALU.mult, op1=ALU.add), [r, cw, o], [o])
            K.op("dve", lambda: nc.vector.scalar_tensor_tensor(out=o[:, :], in0=r[:, 2:SEG + 2], scalar=cw[:, 2, m:m + 1], in1=o[:, :],
                                                                op0=ALU.mult, op1=ALU.add), [r, cw, o], [o])
            return o

        for s0 in starts:
            for g0 in range(s0, s0 + L, SEG):
                zs = zst.next()
                for cc in range(4):
                    o0 = conv(cc, s0, g0)
                    K.dma("act", S["x0c"].t[cc * 128:(cc + 1) * 128, g0:g0 + SEG], o0[:, :], [o0], [S["x0c"]])
                    o1 = conv(4 + cc, s0, g0)
                    o2 = conv(8 + cc, s0, g0)
                    zf = zfb.next()
                    K.op("dve", lambda zf=zf, o1=o1, o2=o2: nc.vector.tensor_tensor(out=zf[:, :], in0=o1[:, :], in1=o2[:, :], op=ALU.mult),
                         [o1, o2], [zf])
                    K.dma("act", S["zf"].t[cc * 128:(cc + 1) * 128, g0:g0 + SEG], zf[:, :], [zf], [S["zf"]])
                    for t8 in range(2):
                        ps = psr.next()
                        psv = ps.t[:, :].bitcast(BF16)

                        def f(zf=zf, psv=psv, t8=t8):
                            for q in range(8):
                                tc_ = t8 * 8 + q
                                ins = nc.tensor.transpose(out=psv[:, q * 128:(q + 1) * 128], in_=zf[:, tc_ * 128:(tc_ + 1) * 128],
                                                          identity=self.identb[:, :])
                            return ins
                        K.op("pe", f, [zf, self.identb], [ps])
                        self.evac(ev, zs[:, t8 * 8:(t8 + 1) * 8, cc * 128:(cc + 1) * 128],
                                  psv[:, :].rearrange("p (q c) -> p q c", c=128), [ps], [zs])
                        ev += 1
                K.dma("sp", S["zT"].t[g0:g0 + SEG, :].rearrange("(q p) c -> p q c", p=128), zs[:, :, :], [zs], [S["zT"]])
        ph.close()

    def phase_hy_dft(self, i, L, starts, cg):
        nc, K, S, I = self.nc, self.K, self.S, self.I
        ph = Phase(K, f"hd{i}_{L}_{cg}")
        PSB = self.PSB
        ns = len(starts)
        NTC = L // 128
        NC = (1 + ns) * 256
        RW = (2 + ns) * 256
        Y = ph.sb("Y", [128, NTC, ns, 2, 256], BF16)
        skip = self.col_load(ph, "skip", I["hy_skip"], I["hy_skip"].t[i].rearrange("(m p) -> p m", p=128), [128, 4])
        phA = Phase(K, f"hdA{i}_{L}_{cg}")
        R = phA.sb("R", [128, NTC, RW], BF16)
        K.dma("sp", R[:, :, 0:256], S["AB"].t[0:L, cg * 256:(cg + 1) * 256].rearrange("(q p) c -> p q c", p=128), [S["AB"]], [R])
        K.dma("sp", R[:, :, (1 + ns) * 256:(2 + ns) * 256],
              S["AB"].t[0:L, 512 + cg * 256:512 + (cg + 1) * 256].rearrange("(q p) c -> p q c", p=128), [S["AB"]], [R])
        for si, s0 in enumerate(starts):
            K.dma("sp", R[:, :, (1 + si) * 256:(2 + si) * 256],
                  S["zT"].t[s0:s0 + L, cg * 256:(cg + 1) * 256].rearrange("(q p) c -> p q c", p=128), [S["zT"]], [R])
        Fc, Fs, Fsi = I[f"fc{L}"], I[f"fs{L}"], I[f"fsi{L}"]
        fcb = Rot([phA.sb(f"fcb{j}", [128, NTC, 128], BF16) for j in range(2)])
        fsb = Rot([phA.sb(f"fsb{j}", [128, NTC, 128], BF16) for j in range(2)])
        csb = Rot([phA.sb(f"csb{j}", [128, NC], F32) for j in range(2)])
        gisb = Rot([phA.sb(f"gisb{j}", [128, 256], F32) for j in range(2)])
        mt = [Rot([phA.sb(f"m{q}_{j}", [128, 256], F32) for j in range(2)]) for q in range(4)]
        gn = phA.sb("gn", [1, 256], F32)
        tn = phA.sb("tn", [1, 256], F32)
        def loadF(kc):
            a, b_ = fcb.next(), fsb.next()
            K.dma("sp", a[:, :, :], Fc.t[:, kc * 128:(kc + 1) * 128].rearrange("(q p) k -> p q k", p=128), [Fc], [a])
            K.dma("sp", b_[:, :, :], Fs.t[:, kc * 128:(kc + 1) * 128].rearrange("(q p) k -> p q k", p=128), [Fs], [b_])
            return a, b_
        nxtF = loadF(0)
        pcs = Rot([(PSB[0], PSB[1]), (PSB[2], PSB[3])])
        pss = Rot([(PSB[4], PSB[5])])
        for kc in range(NTC):
            fa, fb_ = nxtF
            if kc + 1 < NTC:
                nxtF = loadF(kc + 1)
            pc = pcs.next()
            pz = pss.next()

            def f(fa=fa, pc=pc):
                for q in range(NTC):
                    ins = nc.tensor.matmul(pc[0][:, 0:512], fa[:, q, :], R[:, q, 0:512], start=(q == 0), stop=(q == NTC - 1))
                    if NC > 512:
                        ins = nc.tensor.matmul(pc[1][:, 0:NC - 512], fa[:, q, :], R[:, q, 512:NC], start=(q == 0), stop=(q == NTC - 1))
                return ins
            K.op("pe", f, [fa, R], [pc[0], pc[1]] if NC > 512 else [pc[0]])

            def f(fb_=fb_, pz=pz):
                for q in range(NTC):
                    ins = nc.tensor.matmul(pz[0][:, 0:512], fb_[:, q, :], R[:, q, 256:768], start=(q == 0), stop=(q == NTC - 1))
                    if NC > 512:
                        ins = nc.tensor.matmul(pz[1][:, 0:NC - 512], fb_[:, q, :], R[:, q, 768:256 + NC], start=(q == 0), stop=(q == NTC - 1))
                return ins
            K.op("pe", f, [fb_, R], [pz[0], pz[1]] if NC > 512 else [pz[0]])
            if kc == 0:
                def f(fb_=fb_):
                    for q in range(NTC):
                        ins = nc.tensor.matmul(PSB[6][0:1, 0:256], fb_[:, q, 0:1], R[:, q, 0:256], start=(q == 0), stop=(q == NTC - 1))
                    return ins
                K.op("pe", f, [fb_, R], [PSB[6]])
                K.op("act", lambda: nc.scalar.copy(out=gn[0:1, :], in_=PSB[6][0:1, 0:256]), [PSB[6]], [gn])
            cs_ = csb.next()
            K.op("act", lambda cs_=cs_, pc=pc: nc.scalar.copy(out=cs_[:, 0:512], in_=pc[0][:, 0:512]), [pc[0]], [cs_])
            if NC > 512:
                K.op("act", lambda cs_=cs_, pc=pc: nc.scalar.copy(out=cs_[:, 512:NC], in_=pc[1][:, 0:NC - 512]), [pc[1], cs_], [cs_])
            gi_ = gisb.next()
            gcol = ns * 256
            gps = pz[0] if gcol < 512 else pz[1]
            gc0 = gcol % 512
            K.op("act", lambda gi_=gi_, gps=gps: nc.scalar.copy(out=gi_[:, :], in_=gps[:, gc0:gc0 + 256]), [gps], [gi_])
            for si in range(ns):
                zr = cs_[:, (1 + si) * 256:(2 + si) * 256]
                gr = cs_[:, 0:256]
                zcol = si * 256
                zps = pz[0] if zcol < 512 else pz[1]
                zi = zps[:, zcol % 512:zcol % 512 + 256]
                m1, m2, m3, m4 = (r_.next() for r_ in mt)
                K.op("dve", lambda m1=m1, zr=zr, gr=gr: nc.vector.tensor_mul(out=m1[:, :], in0=zr, in1=gr), [cs_], [m1])
                K.op("dve", lambda m2=m2, zi=zi, gi_=gi_: nc.vector.tensor_tensor(out=m2[:, :], in0=zi, in1=gi_[:, :], op=ALU.mult), [zps, gi_], [m2])
                K.op("dve", lambda m3=m3, zr=zr, gi_=gi_: nc.vector.tensor_mul(out=m3[:, :], in0=zr, in1=gi_[:, :]), [cs_, gi_], [m3])
                K.op("dve", lambda m4=m4, zi=zi, gr=gr: nc.vector.tensor_tensor(out=m4[:, :], in0=zi, in1=gr, op=ALU.mult), [zps, cs_], [m4])
                K.op("dve", lambda m1=m1, m2=m2, si=si, kc=kc: nc.vector.tensor_sub(out=Y[:, kc, si, 0, :], in0=m1[:, :], in1=m2[:, :]), [m1, m2], [Y])
                K.op("dve", lambda m3=m3, m4=m4, si=si, kc=kc: nc.vector.tensor_add(out=Y[:, kc, si, 1, :], in0=m3[:, :], in1=m4[:, :]), [m3, m4, Y], [Y])
                if kc == 0:
                    K.op("dve", lambda m1=m1, si=si: nc.vector.tensor_scalar(out=Y[0:1, 0, si, 0, :], in0=m1[0:1, :], scalar1=0.5, scalar2=None, op0=ALU.mult),
                         [m1, Y], [Y])
                    K.op("dve", lambda zi=zi: nc.vector.tensor_tensor(out=tn[0:1, :], in0=zi[0:1, :], in1=gn[0:1, :], op=ALU.mult), [zps, gn], [tn])
                    K.op("dve", lambda si=si: nc.vector.tensor_scalar(out=Y[0:1, 0, si, 1, :], in0=tn[0:1, :], scalar1=0.5, scalar2=None, op0=ALU.mult),
                         [tn, Y], [Y])
        phA.close()
        KB = 8
        ftc = Rot([ph.sb(f"ftc{j}", [128, KB, 512], BF16) for j in range(2)])
        fts = Rot([ph.sb(f"fts{j}", [128, KB, 512], BF16) for j in range(2)])
        zfl = Rot([ph.sb(f"zfl{j}", [128, 512], BF16) for j in range(3)])
        x0l = Rot([ph.sb(f"x0l{j}", [128, 512], F32) for j in range(3)])
        tg = Rot([ph.sb(f"tg{j}", [128, 512], F32) for j in range(2)])
        hyo = Rot([ph.sb(f"hyo{j}", [128, 512], BF16) for j in range(3)])
        accs = [PSB[0], PSB[1], PSB[2], PSB[3]]
        for tb in range(L // 512):
            for kb in range(NTC // KB):
                a, b_ = ftc.next(), fts.next()
                K.dma("sp", a[:, :, :], Fc.t[kb * KB * 128:(kb + 1) * KB * 128, tb * 512:(tb + 1) * 512].rearrange("(q p) t -> p q t", p=128), [Fc], [a])
                K.dma("sp", b_[:, :, :], Fsi.t[kb * KB * 128:(kb + 1) * KB * 128, tb * 512:(tb + 1) * 512].rearrange("(q p) t -> p q t", p=128), [Fsi], [b_])
                for si in range(ns):
                    for ci in range(2):
                        acc = accs[si * 2 + ci]

                        def f(a=a, b_=b_, si=si, ci=ci, acc=acc, kb=kb):
                            for q in range(KB):
                                kc = kb * KB + q
                                nc.tensor.matmul(acc[:, :], Y[:, kc, si, 0, ci * 128:(ci + 1) * 128], a[:, q, :],
                                                 start=(kc == 0), stop=False)
                                ins = nc.tensor.matmul(acc[:, :], Y[:, kc, si, 1, ci * 128:(ci + 1) * 128], b_[:, q, :],
                                                       start=False, stop=(kc == NTC - 1))
                            return ins
                        K.op("pe", f, [a, b_, Y], [acc])
            for si, s0 in enumerate(starts):
                for ci in range(2):
                    c = cg * 2 + ci
                    acc = accs[si * 2 + ci]
                    t0 = s0 + tb * 512
                    zl, xl = zfl.next(), x0l.next()
                    K.dma("sp", zl[:, :], S["zf"].t[c * 128:(c + 1) * 128, t0:t0 + 512], [S["zf"]], [zl])
                    K.dma("sp", xl[:, :], S["x0c"].t[c * 128:(c + 1) * 128, t0:t0 + 512], [S["x0c"]], [xl])
                    t_ = tg.next()
                    K.op("dve", lambda t_=t_, zl=zl, acc=acc, c=c: nc.vector.scalar_tensor_tensor(
                        out=t_[:, :], in0=zl[:, :], scalar=skip[:, c:c + 1], in1=acc[:, :], op0=ALU.mult, op1=ALU.add),
                        [zl, skip, acc], [t_])
                    ho = hyo.next()
                    K.op("dve", lambda ho=ho, t_=t_, xl=xl: nc.vector.tensor_mul(out=ho[:, :], in0=t_[:, :], in1=xl[:, :]), [t_, xl], [ho])
                    K.dma("act", S["mixT"].t[c * 128:(c + 1) * 128, t0:t0 + 512], ho[:, :], [ho], [S["mixT"]])
        ph.close()

    def phase_attn(self, s0, L):
        nc, K, S = self.nc, self.K, self.S
        ph = Phase(K, f"at{s0}")
        PSB = self.PSB
        NKT = L // 128
        scale = float(96 ** -0.5)
        KTt = ph.sb("KTt", [128, 8, L], BF16)
        for h in range(8):
            K.dma("sp", KTt[:, h, :], S["KT"].t[h, :, s0:s0 + L], [S["KT"]], [KTt])
        Vt = ph.sb("Vt", [128, NKT, 1024], BF16)
        K.dma("sp", Vt[:, :, :], S["V"].t[s0:s0 + L, :].rearrange("(q p) c -> p q c", p=128), [S["V"]], [Vt])
        QTb = Rot([ph.sb(f"QTb{j}", [128, 8, 512], BF16) for j in range(2)])
        PT = Rot([ph.sb(f"PT{j}", [128, 512], BF16) for j in range(4)])
        rec = Rot([ph.sb(f"rec{j}", [64, 512], F32) for j in range(2)])
        ato = Rot([ph.sb(f"ato{j}", [64, 512], BF16) for j in range(3)])
        sps = Rot(PSB[0:4])
        ops_ = Rot(PSB[4:6])
        def v_lhsT(kt, h):
            return Vt[:, kt, h * 128:(h + 1) * 128]
        nq = L // 512
        nxt = QTb.next()
        K.dma("sp", nxt[:, :, :], S["QT"].t[:, :, s0:s0 + 512].rearrange("h r t -> r h t"), [S["QT"]], [nxt])
        for qb in range(nq):
            qt = nxt
            if qb + 1 < nq:
                nxt = QTb.next()
                K.dma("sp", nxt[:, :, :], S["QT"].t[:, :, s0 + (qb + 1) * 512:s0 + (qb + 2) * 512].rearrange("h r t -> r h t"),
                      [S["QT"]], [nxt])
            for h in range(8):
                oacc = ops_.next()
                pend = []

                def pv(kt, p_, oacc=oacc, h=h):
                    K.op("pe", lambda: nc.tensor.matmul(oacc[:, :], v_lhsT(kt, h), p_[:, :], start=(kt == 0), stop=(kt == NKT - 1)),
                         [Vt, p_], [oacc])
                for kt in range(NKT):
                    sp_ = sps.next()
                    K.op("pe", lambda sp_=sp_, kt=kt: nc.tensor.matmul(sp_[:, :], KTt[:, h, kt * 128:(kt + 1) * 128], qt[:, h, :],
                                                                        start=True, stop=True), [KTt, qt], [sp_])
                    p_ = PT.next()
                    K.op("act", lambda sp_=sp_, p_=p_: nc.scalar.activation(out=p_[:, :], in_=sp_[:, :], func=AF.Exp, scale=scale),
                         [sp_], [p_])
                    pend.append((kt, p_))
                    if len(pend) > 2:
                        pv(*pend.pop(0))
                while pend:
                    pv(*pend.pop(0))
                r_ = rec.next()
                K.op("dve", lambda r_=r_, oacc=oacc: nc.vector.reciprocal(out=r_[:, :], in_=oacc[64:128, :]), [oacc], [r_])
                a_ = ato.next()
                K.op("dve", lambda a_=a_, r_=r_, oacc=oacc: nc.vector.tensor_tensor(out=a_[:, :], in0=oacc[0:64, :], in1=r_[:, :], op=ALU.mult),
                     [oacc, r_], [a_])
                t0 = s0 + qb * 512
                K.dma("sp", S["mixT"].t[512 + h * 64:512 + (h + 1) * 64, t0:t0 + 512], a_[:, :], [a_], [S["mixT"]])
        ph.close()

    def phase_odd_in(self, i):
        nc, K, S, I = self.nc, self.K, self.S, self.I
        ph = Phase(K, f"oa{i}")
        PSB = self.PSB
        win = ph.sb("win", [128, 8, 1536], BF16)
        wb = self.WB[("od_w_in", i)]
        for k in range(8):
            K.dma("sp", win[:, k, :], wb.t[k * 128:(k + 1) * 128, :], [wb], [win])
        bin_ = self.col_load(ph, "bin", I["od_b_in"], I["od_b_in"].t[i].rearrange("(m p) -> p m", p=128), [128, 12])
        xTb = Rot([ph.sb(f"xTb{j}", [128, 8, 512], BF16) for j in range(2)])
        af = Rot([ph.sb(f"af{j}", [128, 512], F32) for j in range(2)])
        sg = Rot([ph.sb(f"sg{j}", [128, 512], F32) for j in range(2)])
        hg = Rot([ph.sb(f"hg{j}", [128, 512], BF16) for j in range(3)])
        pu = Rot([ph.sb(f"pu{j}", [128, 512], F32) for j in range(3)])
        psr = Rot(PSB[0:4])
        nblk = NT // 512
        nxt = xTb.next()
        K.dma("sp", nxt[:, :, :], S["xT"].t[:, 0:512].rearrange("(k p) t -> p k t", p=128), [S["xT"]], [nxt])

        def proj(m, xtb):
            ps = psr.next()

            def f():
                for k in range(8):
                    ins = nc.tensor.matmul(ps[:, :], win[:, k, m * 128:(m + 1) * 128], xtb[:, k, :], start=(k == 0), stop=(k == 7))
                return ins
            K.op("pe", f, [win, xtb], [ps])
            return ps
        for b in range(nblk):
            T0 = b * 512
            xtb = nxt
            if b + 1 < nblk:
                nxt = xTb.next()
                K.dma("sp", nxt[:, :, :], S["xT"].t[:, T0 + 512:T0 + 1024].rearrange("(k p) t -> p k t", p=128), [S["xT"]], [nxt])
            for c in range(4):
                pa = proj(c, xtb)
                a_ = af.next()
                K.op("act", lambda a_=a_, pa=pa, c=c: nc.scalar.activation(out=a_[:, :], in_=pa[:, :], func=AF.Identity, bias=bin_[:, c:c + 1], scale=1.0),
                     [pa, bin_], [a_])
                pg = proj(4 + c, xtb)
                s_ = sg.next()
                K.op("act", lambda s_=s_, pg=pg, c=c: nc.scalar.activation(out=s_[:, :], in_=pg[:, :], func=AF.Sigmoid, bias=bin_[:, 4 + c:5 + c], scale=1.0),
                     [pg, bin_], [s_])
                h_ = hg.next()
                K.op("dve", lambda h_=h_, a_=a_, s_=s_: nc.vector.tensor_tensor(out=h_[:, :], in0=a_[:, :], in1=s_[:, :], op=ALU.mult), [a_, s_], [h_])
                K.dma("sp", S["hgl"].t[c * 128:(c + 1) * 128, T0:T0 + 512], h_[:, :], [h_], [S["hgl"]])
            for c in range(4):
                pp = proj(8 + c, xtb)
                p_ = pu.next()
                K.op("act", lambda p_=p_, pp=pp, c=c: nc.scalar.activation(out=p_[:, :], in_=pp[:, :], func=AF.Identity, bias=bin_[:, 8 + c:9 + c], scale=1.0),
                     [pp, bin_], [p_])
                K.dma("sp", S["u"].t[c * 128:(c + 1) * 128, T0:T0 + 512], p_[:, :], [p_], [S["u"]])
        ph.close()

    def phase_conformer(self, i):
        nc, K, S, I = self.nc, self.K, self.S, self.I
        ph = Phase(K, f"cf{i}")
        PSB = self.PSB
        wraw = ph.sb("wraw", [31, 512], F32)
        K.dma("sp", wraw[:, :], I["cf_dw_w"].t[i], [I["cf_dw_w"]], [wraw])
        wcol = ph.sb("wcol", [128, 4, 31], F32)
        for c in range(4):
            ps = PSB[c]
            K.op("pe", lambda ps=ps, c=c: nc.tensor.transpose(out=ps[:, 0:31], in_=wraw[:, c * 128:(c + 1) * 128], identity=self.identf[0:31, 0:31]),
                 [wraw, self.identf], [ps])
            K.op("dve", lambda ps=ps, c=c: nc.vector.tensor_copy(out=wcol[:, c, :], in_=ps[:, 0:31]), [ps], [wcol])
        Dg = ph.sb("Dg", [128, 4, 31, 128], BF16)
        for c in range(4):
            for tp in range(31):
                eng = "dve"
                if eng == "dve":
                    K.op("dve", lambda c=c, tp=tp: nc.vector.tensor_scalar(out=Dg[:, c, tp, :], in0=self.identf[:, :], scalar1=wcol[:, c, tp:tp + 1],
                                                                           scalar2=None, op0=ALU.mult), [self.identf, wcol], [Dg])
                else:
                    K.op("dve", lambda c=c, tp=tp: nc.vector.tensor_scalar(out=Dg[:, c, tp, :], in0=self.identf[:, :], scalar1=wcol[:, c, tp:tp + 1],
                                                                            scalar2=None, op0=ALU.mult), [self.identf, wcol], [Dg])
        dwb = self.bcast_row(ph, "dwb", I["cf_dw_b"], I["cf_dw_b"].t[i:i + 1, :], 512)
        lg = self.bcast_row(ph, "lg", I["cf_ln_g"], I["cf_ln_g"].t[i:i + 1, :], 512)
        lb = self.bcast_row(ph, "lb", I["cf_ln_b"], I["cf_ln_b"].t[i:i + 1, :], 512)
        hw_ = Rot([ph.sb(f"hw{j}", [128, 4, 542], BF16) for j in range(2)])
        cvo = Rot([ph.sb(f"cvo{j}", [128, 512], F32) for j in range(2)])
        lno = Rot([ph.sb(f"lno{j}", [128, 512], F32) for j in range(2)])
        cfb = Rot([ph.sb(f"cfb{j}", [128, 512], BF16) for j in range(2)])
        cfT = Rot([ph.sb(f"cfT{j}", [128, 4, 512], BF16) for j in range(2)])
        tmp = {"st": ph.sb("st", [128, 12], F32), "mv": ph.sb("mv", [128, 2], F32), "rs": ph.sb("rs", [128, 1], F32), "nm": ph.sb("nm", [128, 1], F32)}
        pacc = Rot(PSB[0:3])
        ptr = Rot(PSB[4:6])
        ev = 0
        for b in range(NT // 512):
            T0 = b * 512
            s0, L = seq_of(T0)
            hb_ = hw_.next()
            lo, hi = T0 - 15, T0 + 512 + 15
            a, e = max(lo, s0), min(hi, s0 + L)
            if a > lo:
                K.op("dve", lambda hb_=hb_: nc.vector.memset(hb_[:, :, 0:15], 0.0), [], [hb_])
            if e < hi:
                K.op("dve", lambda hb_=hb_: nc.vector.memset(hb_[:, :, 527:542], 0.0), [], [hb_])
            K.dma("sp", hb_[:, :, a - lo:e - lo], S["hgl"].t[:, a:e].rearrange("(c p) t -> p c t", p=128), [S["hgl"]], [hb_])
            ct = cfT.next()
            for j in range(4):
                acc = pacc.next()

                def f(j=j, acc=acc, hb_=hb_):
                    for c in range(4):
                        for tp in range(31):
                            ins = nc.tensor.matmul(acc[:, c * 128:(c + 1) * 128], hb_[:, c, j * 128 + tp:j * 128 + tp + 128], Dg[:, c, tp, :],
                                                   start=(tp == 0), stop=(tp == 30))
                    return ins
                K.op("pe", f, [hb_, Dg], [acc])
                co = cvo.next()
                K.op("dve", lambda co=co, acc=acc: nc.vector.tensor_tensor(out=co[:, :], in0=acc[:, :], in1=dwb[:, :], op=ALU.add), [acc, dwb], [co])
                lo_ = lno.next()
                self.layer_norm_tile(ph, tmp, co, lg, lb, lo_, 512, "cf")
                cb_ = cfb.next()
                K.op("act", lambda cb_=cb_, lo_=lo_: nc.scalar.activation(out=cb_[:, :], in_=lo_[:, :], func=AF.Silu), [lo_], [cb_])
                pt = ptr.next()
                self.transpose_to(cb_, 4, pt, lambda ct=ct, j=j: ct[:, :, j * 128:(j + 1) * 128], ct, ev)
                ev += 1
            K.dma("act", S["mixT"].t[0:512, T0:T0 + 512].rearrange("(c p) t -> p c t", p=128), ct[:, :, :], [ct], [S["mixT"]])
        ph.close()

    def phase_pool(self, i):
        nc, K, S, I = self.nc, self.K, self.S, self.I
        ph = Phase(K, f"pl{i}")
        PSB = self.PSB
        pw = ph.sb("pw", [128, 4, 128], BF16)
        wb = self.WB[("pool_w", i)]
        K.dma("sp", pw[:, :, :], wb.t[:, :].rearrange("(g a) b -> a g b", a=128), [wb], [pw])
        psc = self.col_load(ph, "psc", I["pool_scale"], I["pool_scale"].t[i].rearrange("(m p) -> p m", p=128), [128, 4])
        edge = ph.sb("edge", [128, 64], F32)
        K.dma("sp", edge[:, :], I["pooledge"].t[:, :], [I["pooledge"]], [edge])
        LM = 4096
        ub = Rot([ph.sb(f"ub{j}", [128, LM + 32], F32) for j in range(2)])
        sa = ph.sb("sa", [128, LM + 32], F32)
        sb_ = ph.sb("sb", [128, LM + 32], F32)
        db = Rot([ph.sb(f"db{j}", [128, LM], BF16) for j in range(2)])
        df = ph.sb("df", [128, LM], F32)
        po = Rot([ph.sb(f"po{j}", [128, 512], BF16) for j in range(3)])
        psr = Rot(PSB[0:4])
        for s0, L in SEQS:
            for g, w in enumerate((2, 4, 8, 16)):
                lo, hi = w // 2, w - 1 - w // 2
                u_ = ub.next()
                K.op("dve", lambda u_=u_: nc.vector.memset(u_[:, 0:16], 0.0), [], [u_])
                K.op("dve", lambda u_=u_, L=L: nc.vector.memset(u_[:, 16 + L:32 + L], 0.0), [], [u_])
                K.dma("sp", u_[:, 16:16 + L], S["u"].t[g * 128:(g + 1) * 128, s0:s0 + L], [S["u"]], [u_])
                W = L + 32
                cur = u_
                step = 1
                k_ = 0
                outs = [sa, sb_]
                while step < w:
                    o = outs[k_ % 2]
                    st_ = 2 * step - 1
                    eng = "dve"
                    if eng == "dve":
                        K.op("dve", lambda o=o, cur=cur, st_=st_, step=step, W=W: nc.vector.tensor_tensor(
                            out=o[:, st_:W], in0=cur[:, st_:W], in1=cur[:, st_ - step:W - step], op=ALU.add), [cur], [o])
                    else:
                        K.op("dve", lambda o=o, cur=cur, st_=st_, step=step, W=W: nc.vector.tensor_add(
                            out=o[:, st_:W], in0=cur[:, st_:W], in1=cur[:, st_ - step:W - step]), [cur], [o])
                    cur = o
                    step *= 2
                    k_ += 1
                K.op("dve", lambda cur=cur, u_=u_, hi=hi, w=w, L=L: nc.vector.scalar_tensor_tensor(
                    out=df[:, 0:L], in0=cur[:, 16 + hi:16 + hi + L], scalar=1.0 / w, in1=u_[:, 16:16 + L], op0=ALU.mult, op1=ALU.subtract),
                    [cur, u_], [df])
                K.op("dve", lambda cur=cur, hi=hi, g=g: nc.vector.tensor_tensor(out=df[:, 0:8], in0=cur[:, 16 + hi:24 + hi], in1=edge[:, g * 16:g * 16 + 8], op=ALU.mult),
                     [cur, edge, df], [df])
                K.op("dve", lambda u_=u_: nc.vector.tensor_sub(out=df[:, 0:8], in0=df[:, 0:8], in1=u_[:, 16:24]), [u_, df], [df])
                for r in range(8):
                    t_ = L - 1 - r
                    K.op("dve", lambda cur=cur, t_=t_, hi=hi, g=g, r=r, u_=u_: nc.vector.scalar_tensor_tensor(
                        out=df[:, t_:t_ + 1], in0=cur[:, 16 + hi + t_:17 + hi + t_], scalar=edge[:, g * 16 + 8 + r:g * 16 + 9 + r],
                        in1=u_[:, 16 + t_:17 + t_], op0=ALU.mult, op1=ALU.subtract), [cur, edge, u_, df], [df])
                d_ = db.next()
                K.op("act", lambda d_=d_, L=L: nc.scalar.copy(out=d_[:, 0:L], in_=df[:, 0:L]), [df], [d_])
                for tb in range(L // 512):
                    ps = psr.next()
                    K.op("pe", lambda ps=ps, d_=d_, tb=tb, g=g: nc.tensor.matmul(ps[:, :], pw[:, g, :], d_[:, tb * 512:(tb + 1) * 512], start=True, stop=True),
                         [pw, d_], [ps])
                    o_ = po.next()
                    K.op("act", lambda o_=o_, ps=ps, g=g: nc.scalar.activation(out=o_[:, :], in_=ps[:, :], func=AF.Copy, scale=psc[:, g:g + 1]),
                         [ps, psc], [o_])
                    t0 = s0 + tb * 512
                    K.dma("act", S["mixT"].t[512 + g * 128:512 + (g + 1) * 128, t0:t0 + 512], o_[:, :], [o_], [S["mixT"]])
        ph.close()

    def phase_tail1(self, l, xcur):
        nc, K, S, I = self.nc, self.K, self.S, self.I
        ph = Phase(K, f"ta{l}")
        PSB = self.PSB
        i = l // 2
        odd = (l % 2 == 1)
        wo = ph.sb("wo", [128, 8, 1024], BF16)
        wob = self.WB[("od_w_out" if odd else "ev_w_out", i)]
        K.dma("sp", wo[:, :, :], wob.t[:, :].rearrange("(k p) n -> p k n", p=128), [wob], [wo])
        g1 = self.bcast_row(ph, "g1", I["ln1_g"], I["ln1_g"].t[l:l + 1, :], 1024)
        b1 = self.bcast_row(ph, "b1", I["ln1_b"], I["ln1_b"].t[l:l + 1, :], 1024)
        if odd:
            bo = ph.sb("bo", [1, 1024], F32)
            K.dma("sp", bo[:, :], I["od_b_out"].t[i:i + 1, :], [I["od_b_out"]], [bo])
            onesr = ph.sb("onesr", [1, 128], F32)
            K.op("dve", lambda: nc.vector.memset(onesr[:, :], 1.0), [], [onesr])
        mxb = Rot([ph.sb(f"mxb{j}", [128, 8, 512], BF16) for j in range(2)])
        xin = Rot([ph.sb(f"xin{j}", [128, 1024], F32) for j in range(6)])
        tt = Rot([ph.sb(f"tt{j}", [128, 1024], F32) for j in range(2)])
        xmid = Rot([ph.sb(f"xmid{j}", [128, 1024], F32) for j in range(3)])
        xmb = Rot([ph.sb(f"xmb{j}", [128, 1024], BF16) for j in range(2)])
        xmT = Rot([ph.sb(f"xmT{j}", [128, 8, 512], BF16) for j in range(2)])
        tmp = {"st": ph.sb("st", [128, 12], F32), "mv": ph.sb("mv", [128, 2], F32), "rs": ph.sb("rs", [128, 1], F32), "nm": ph.sb("nm", [128, 1], F32)}
        accs = Rot([(PSB[0], PSB[1]), (PSB[2], PSB[3])])
        tps = Rot(PSB[4:8])
        nblk = NT // 512
        ev = 0

        def load_blk(b):
            m = mxb.next()
            K.dma("sp", m[:, :, :], S["mixT"].t[:, b * 512:(b + 1) * 512].rearrange("(k p) t -> p k t", p=128), [S["mixT"]], [m])
            xs = []
            for j in range(4):
                x_ = xin.next()
                t0 = b * 512 + j * 128
                K.dma("sp", x_[:, :], xcur.t[t0:t0 + 128, :], [xcur], [x_])
                xs.append(x_)
            return m, xs
        nxt = load_blk(0)
        for b in range(nblk):
            mx, xs = nxt
            xT_ = xmT.next()
            for j in range(4):
                acc = accs.next()

                def f(j=j, acc=acc, mx=mx):
                    for hlf in range(2):
                        for k in range(8):
                            ins = nc.tensor.matmul(acc[hlf][:, :], mx[:, k, j * 128:(j + 1) * 128], wo[:, k, hlf * 512:(hlf + 1) * 512],
                                                   start=(k == 0), stop=(k == 7 and not odd))
                        if odd:
                            ins = nc.tensor.matmul(acc[hlf][:, :], onesr[0:1, :], bo[0:1, hlf * 512:(hlf + 1) * 512], start=False, stop=True)
                    return ins
                K.op("pe", f, [mx, wo] + ([onesr, bo] if odd else []), [acc[0], acc[1]])
                xm_ = xmid.next()
                self.resid_ln(ph, tmp, tt, acc, xs[j], g1, b1, xm_)
                t0 = b * 512 + j * 128
                K.dma("act", S["xM"].t[t0:t0 + 128, :], xm_[:, :], [xm_], [S["xM"]])
                xb_ = xmb.next()
                K.op("act", lambda xb_=xb_, xm_=xm_: nc.scalar.copy(out=xb_[:, :], in_=xm_[:, :]), [xm_], [xb_])
                pt = tps.next()
                self.transpose_to(xb_, 8, pt, lambda j=j, xT_=xT_: xT_[:, :, j * 128:(j + 1) * 128], xT_, ev)
                ev += 1
            if b + 1 < nblk:
                nxt = load_blk(b + 1)
            K.dma("act", S["xmT"].t[:, b * 512:(b + 1) * 512].rearrange("(k p) t -> p k t", p=128), xT_[:, :, :], [xT_], [S["xmT"]])
        ph.close()

    def resid_ln(self, ph, tmp, tt, acc, x_, g_, b_, out_):
        nc, K = self.nc, self.K
        t_ = tt.next()
        for hlf in range(2):
            K.op("dve", lambda hlf=hlf: nc.vector.scalar_tensor_tensor(out=t_[:, hlf * 512:(hlf + 1) * 512], in0=x_[:, hlf * 512:(hlf + 1) * 512],
                                                                       scalar=ALPHA, in1=acc[hlf][:, :], op0=ALU.mult, op1=ALU.add),
                 [x_, acc[hlf]], [t_])
        self.layer_norm_tile(ph, tmp, t_, g_, b_, out_, 1024, "t")

    def phase_tail2(self, l, xnext, last):
        nc, K, S, I = self.nc, self.K, self.S, self.I
        ph = Phase(K, f"tb{l}")
        PSB = self.PSB
        w2 = ph.sb("w2", [128, 32, 1024], BF16)
        w2b = self.WB[("mlp_w2", l)]
        for f4 in range(4):
            K.dma("sp", w2[:, f4 * 8:(f4 + 1) * 8, :], w2b.t[f4 * 1024:(f4 + 1) * 1024, :].rearrange("(k p) n -> p k n", p=128), [w2b], [w2])
        w1b = self.WB[("mlp_w1", l)]
        g2 = self.bcast_row(ph, "g2", I["ln2_g"], I["ln2_g"].t[l:l + 1, :], 1024)
        b2 = self.bcast_row(ph, "b2", I["ln2_b"], I["ln2_b"].t[l:l + 1, :], 1024)
        xmT = Rot([ph.sb(f"xmT{j}", [128, 8, 512], BF16) for j in range(2)])
        xmid = Rot([ph.sb(f"xmid{j}", [128, 1024], F32) for j in range(5)])
        tt = Rot([ph.sb(f"tt{j}", [128, 1024], F32) for j in range(1)])
        hT = ph.sb("hT", [128, 32, 512], BF16)
        w1s = Rot([ph.sb(f"w1s{j}", [128, 8, 512], BF16) for j in range(2)])
        rl = Rot([ph.sb(f"rl{j}", [128, 512], F32) for j in range(2)])
        xo = Rot([ph.sb(f"xo{j}", [128, 1024], F32) for j in range(2)])
        xob = Rot([ph.sb(f"xob{j}", [128, 1024], BF16) for j in range(2)])
        xts = Rot([ph.sb(f"xts{j}", [128, 8, 512], BF16) for j in range(1)])
        tmp = {"st": ph.sb("st", [128, 12], F32), "mv": ph.sb("mv", [128, 2], F32), "rs": ph.sb("rs", [128, 1], F32), "nm": ph.sb("nm", [128, 1], F32)}
        accs = Rot([(PSB[0], PSB[1]), (PSB[2], PSB[3])])
        hps = Rot(PSB[4:6])
        tps = Rot(PSB[6:8])
        nblk = NT // 512
        ev = 0

        def load_xT(b):
            m = xmT.next()
            K.dma("sp", m[:, :, :], S["xmT"].t[:, b * 512:(b + 1) * 512].rearrange("(k p) t -> p k t", p=128), [S["xmT"]], [m])
            return m
        nxt = load_xT(0)
        for b in range(nblk):
            xT_ = nxt
            if b + 1 < nblk:
                nxt = load_xT(b + 1)
            for fg in range(8):
                w1_ = w1s.next()
                K.dma("sp", w1_[:, :, :], w1b.t[:, fg * 512:(fg + 1) * 512].rearrange("(k p) n -> p k n", p=128), [w1b], [w1_])
                for fc in range(4):
                    hp = hps.next()

                    def f(hp=hp, w1_=w1_, fc=fc, xT_=xT_):
                        for k in range(8):
                            ins = nc.tensor.matmul(hp[:, :], w1_[:, k, fc * 128:(fc + 1) * 128], xT_[:, k, :], start=(k == 0), stop=(k == 7))
                        return ins
                    K.op("pe", f, [w1_, xT_], [hp])
                    r_ = rl.next()
                    K.op("act", lambda r_=r_, hp=hp: nc.scalar.activation(out=r_[:, :], in_=hp[:, :], func=AF.Relu), [hp], [r_])
                    fi = fg * 4 + fc
                    if fi % 2 == 0:
                        K.op("dve", lambda r_=r_, fi=fi: nc.vector.tensor_tensor(out=hT[:, fi, :], in0=r_[:, :], in1=r_[:, :], op=ALU.mult), [r_], [hT])
                    else:
                        K.op("dve", lambda r_=r_, fi=fi: nc.vector.tensor_mul(out=hT[:, fi, :], in0=r_[:, :], in1=r_[:, :]), [r_], [hT])
            dst = xts.next()
            for j in range(4):
                t0 = b * 512 + j * 128
                xm_ = xmid.next()
                K.dma("sp", xm_[:, :], S["xM"].t[t0:t0 + 128, :], [S["xM"]], [xm_])
                acc = accs.next()

                def f(j=j, acc=acc):
                    for hlf in range(2):
                        for fi in range(32):
                            ins = nc.tensor.matmul(acc[hlf][:, :], hT[:, fi, j * 128:(j + 1) * 128], w2[:, fi, hlf * 512:(hlf + 1) * 512],
                                                   start=(fi == 0), stop=(fi == 31))
                    return ins
                K.op("pe", f, [hT, w2], [acc[0], acc[1]])
                xo_ = xo.next()
                self.resid_ln(ph, tmp, tt, acc, xm_, g2, b2, xo_)
                K.dma("act", xnext.t[t0:t0 + 128, :], xo_[:, :], [xo_], [xnext])
                if not last:
                    xb_ = xob.next()
                    K.op("act", lambda xb_=xb_, xo_=xo_: nc.scalar.copy(out=xb_[:, :], in_=xo_[:, :]), [xo_], [xb_])
                    pt = tps.next()
                    self.transpose_to(xb_, 8, pt, lambda j=j, dst=dst: dst[:, :, j * 128:(j + 1) * 128], dst, ev)
                    ev += 1
            if not last:
                K.dma("act", S["xT"].t[:, b * 512:(b + 1) * 512].rearrange("(k p) t -> p k t", p=128), dst[:, :, :], [dst], [S["xT"]])
        ph.close()


_PROG = {}


def get_prog(**kw):
    key = tuple(sorted((k, str(v)) for k, v in kw.items()))
    if key not in _PROG:
        _PROG[key] = Prog(**kw)
    return _PROG[key]


def make_in_maps(inputs, nlayers=DEPTH):
    c = host_consts()
    xp = np.asarray(inputs["x_prompt"], dtype=np.float32)
    xs = np.asarray(inputs["x_sample"], dtype=np.float32)
    maps = []
    for core in range(8):
        m = {}
        m["xin"] = np.ascontiguousarray(np.concatenate([xp[core // 2], xs[2 * core], xs[2 * core + 1]], axis=0))
        for n in WEIGHT_SHAPES:
            m[n] = np.ascontiguousarray(np.asarray(inputs[n], dtype=np.float32)[:wlead(n, nlayers)])
        for n in CONST_SHAPES:
            m[n] = c[n]
        maps.append(m)
    return maps


def kernel(**inputs):
    prog = get_prog()
    maps = make_in_maps(inputs)
    res = run_bass_kernel_spmd(prog.nc, maps, core_ids=list(range(8)))
    yp = np.zeros((4, 4096, D), np.float32)
    ys = np.zeros((16, 2048, D), np.float32)
    for core in range(8):
        y = np.asarray(res.results[core]["y"]).reshape(NT, D)
        half = core % 2
        yp[core // 2, half * 2048:(half + 1) * 2048] = y[half * 2048:(half + 1) * 2048]
        ys[2 * core] = y[4096:6144]
        ys[2 * core + 1] = y[6144:8192]
    return (yp, ys)
```
